# Optimizing a Trainium2 kernel written in Bass

```python
import jax, jax.numpy as jnp
from jax import lax
import numpy as np

D_MODEL = 1024
BATCH = 8
SEQ = 8192
DEPTH = 1
DEC_BATCH = 16
DEC_SEQ = 2048
PAST_LEN = 128

MLA_HEADS = 8
QK_NOPE = 64
QK_ROPE = 32
V_HEAD = 64
Q_LORA = 384
KV_LORA = 256
MLA_WIDTH = MLA_HEADS * V_HEAD
ROPE_BASE = 10000.0
Q_BLOCK = 128
RWKV_HEADS = 8
RWKV_HEAD = 64
RWKV_WIDTH = RWKV_HEADS * RWKV_HEAD
DECAY_LORA = 64
ICLR_LORA = 64
GATE_LORA = 128
N_BRANCH = 2
D_FF = ((8 * D_MODEL // 3 + 255) // 256) * 256
EPS = 1e-6
LNX_EPS = 64e-5
MLA_IN = Q_LORA + KV_LORA + QK_ROPE
RWKV_IN = 3 * RWKV_WIDTH + DECAY_LORA + ICLR_LORA + GATE_LORA
GATE_IN = N_BRANCH * D_MODEL
IN_COLS = MLA_IN + RWKV_IN + GATE_IN

kernel_name = "hybrid_mla_rwkv7_adaln_encoder"


def rmsnorm(x, g):
    xf = x.astype(jnp.float32)
    y = xf * lax.rsqrt(jnp.mean(xf * xf, axis=-1, keepdims=True) + EPS)
    return (y * g.astype(jnp.float32)).astype(x.dtype)


def rope_tables(T):
    half = QK_ROPE // 2
    inv = ROPE_BASE ** (-jnp.arange(half, dtype=jnp.float32) / half)
    ang = jnp.arange(T, dtype=jnp.float32)[:, None] * inv[None, :]
    return jnp.cos(ang), jnp.sin(ang)


def apply_rope(x, cos, sin):
    xf = x.astype(jnp.float32)
    x1, x2 = jnp.split(xf, 2, axis=-1)
    return jnp.concatenate([x1 * cos - x2 * sin, x2 * cos + x1 * sin], axis=-1).astype(x.dtype)


def centred_shift(p):
    pad = jnp.pad(p, ((0, 0), (1, 1), (0, 0)))
    return 0.5 * (pad[:, :-2] + pad[:, 2:])


def block_attention(q, k, v):
    B, T, H, dqk = q.shape
    scale = dqk ** -0.5
    nblk = T // Q_BLOCK
    qb = q.reshape(B, nblk, Q_BLOCK, H, dqk).transpose(1, 0, 2, 3, 4)

    def one_block(qblk):
        s = jnp.einsum('bqhd,bkhd->bhqk', qblk, k).astype(jnp.float32) * scale
        pr = jax.nn.softmax(s, axis=-1).astype(v.dtype)
        return jnp.einsum('bhqk,bkhd->bqhd', pr, v)

    out = lax.map(one_block, qb)
    return out.transpose(1, 0, 2, 3, 4).reshape(B, T, H * v.shape[-1])


def mla_branch(p_mla, q_a_norm, kv_a_norm, w_uq, w_ukv):
    B, T, _ = p_mla.shape
    cq = rmsnorm(p_mla[..., :Q_LORA], q_a_norm)
    ckv = rmsnorm(p_mla[..., Q_LORA:Q_LORA + KV_LORA], kv_a_norm)
    k_pe = p_mla[..., Q_LORA + KV_LORA:]
    q = (cq @ w_uq).reshape(B, T, MLA_HEADS, QK_NOPE + QK_ROPE)
    kv = (ckv @ w_ukv).reshape(B, T, MLA_HEADS, QK_NOPE + V_HEAD)
    q_nope, q_pe = q[..., :QK_NOPE], q[..., QK_NOPE:]
    k_nope, v = kv[..., :QK_NOPE], kv[..., QK_NOPE:]
    cos, sin = rope_tables(T)
    q_pe = apply_rope(q_pe, cos[:, None, :], sin[:, None, :])
    k_pe = apply_rope(k_pe, cos, sin)
    q = jnp.concatenate([q_nope, q_pe], axis=-1)
    k = jnp.concatenate([k_nope, jnp.broadcast_to(k_pe[:, :, None, :], (B, T, MLA_HEADS, QK_ROPE))], axis=-1)
    return block_attention(q, k, v)


def wkv_scan(r, w, k, v, kk, a, reverse):
    B, T, H, N = r.shape
    xs = tuple(t.transpose(1, 0, 2, 3) for t in (r, w, k, v, kk, a))

    def step(S, inp):
        r_t, w_t, k_t, v_t, kk_t, a_t = inp
        sa = jnp.einsum('bhvk,bhk->bhv', S, -kk_t)
        S = (S * w_t[:, :, None, :]
             + sa[..., :, None] * (kk_t * a_t)[..., None, :]
             + v_t[..., :, None] * k_t[..., None, :])
        y = jnp.einsum('bhvk,bhk->bhv', S, r_t)
        return S, y

    S0 = jnp.zeros((B, H, N, N), jnp.float32)
    _, ys = lax.scan(step, S0, xs, reverse=reverse)
    return ys.transpose(1, 0, 2, 3)


def rwkv_branch(p_rwkv, mu_shift, w0, w_decay_up, a0, w_iclr_up, w_gate_up, k_k, k_a, r_k, lnx_w, lnx_b):
    B, T, _ = p_rwkv.shape
    pf = p_rwkv.astype(jnp.float32)
    xs = pf + mu_shift.astype(jnp.float32) * (centred_shift(pf) - pf)
    W = RWKV_WIDTH
    r = xs[..., :W]
    k = xs[..., W:2 * W]
    v = xs[..., 2 * W:3 * W]
    wd = xs[..., 3 * W:3 * W + DECAY_LORA]
    ad = xs[..., 3 * W + DECAY_LORA:3 * W + DECAY_LORA + ICLR_LORA]
    gd = xs[..., 3 * W + DECAY_LORA + ICLR_LORA:]
    f32 = lambda t: t.astype(jnp.float32)
    g = jax.nn.sigmoid(gd) @ f32(w_gate_up)
    heads = lambda t: t.reshape(B, T, RWKV_HEADS, RWKV_HEAD)
    kk = heads(k * f32(k_k))
    kk = kk * lax.rsqrt(jnp.sum(kk * kk, axis=-1, keepdims=True) + 1e-12)
    tw = jnp.tanh(wd)
    rh, vh = heads(r), heads(v)
    y = jnp.zeros((B, T, RWKV_HEADS, RWKV_HEAD), jnp.float32)
    bonus_k = jnp.zeros((B, T, RWKV_HEADS, RWKV_HEAD), jnp.float32)
    for d in range(2):
        wlog = -jax.nn.softplus(-(f32(w0[d]) + tw @ f32(w_decay_up[d]))) - 0.5
        decay = jnp.exp(-jnp.exp(wlog))
        a = jax.nn.sigmoid(f32(a0[d]) + ad @ f32(w_iclr_up[d]))
        k_d = k * (1.0 + (a - 1.0) * f32(k_a))
        y = y + wkv_scan(rh, heads(decay), heads(k_d), vh, kk, heads(a), reverse=(d == 1))
        bonus_k = bonus_k + heads(k_d)
    mu = jnp.mean(y, axis=-1, keepdims=True)
    var = jnp.mean(jnp.square(y - mu), axis=-1, keepdims=True)
    yn = ((y - mu) * lax.rsqrt(var + LNX_EPS)).reshape(B, T, W) * f32(lnx_w) + f32(lnx_b)
    bonus = jnp.sum(rh * bonus_k * f32(r_k), axis=-1, keepdims=True) * vh
    out = (yn + bonus.reshape(B, T, W)) * g
    return out.astype(p_rwkv.dtype)


def encoder(x, c, w_ada, b_ada, norm_mix, w_in, q_a_norm, kv_a_norm, w_uq, w_ukv,
            mu_shift, w0, w_decay_up, a0, w_iclr_up, w_gate_up, k_k, k_a, r_k, lnx_w, lnx_b,
            w_mla_o, w_rwkv_o, w_out, norm_ffn, w_ffn_in, w_ffn_out, final_norm):
    for l in range(DEPTH):
        mod = jax.nn.silu(c) @ w_ada[l] + b_ada[l]
        sh1, sc1, g1, sh2, sc2, g2 = jnp.split(mod[:, None, :], 6, axis=-1)
        h = rmsnorm(x, norm_mix[l]) * (1.0 + sc1) + sh1
        p = h @ w_in[l]
        p_mla = p[..., :MLA_IN]
        p_rwkv = p[..., MLA_IN:MLA_IN + RWKV_IN]
        p_gate = p[..., MLA_IN + RWKV_IN:]
        o_mla = mla_branch(p_mla, q_a_norm[l], kv_a_norm[l], w_uq[l], w_ukv[l]) @ w_mla_o[l]
        o_rwkv = rwkv_branch(p_rwkv, mu_shift[l], w0[l], w_decay_up[l], a0[l], w_iclr_up[l],
                             w_gate_up[l], k_k[l], k_a[l], r_k[l], lnx_w[l], lnx_b[l]) @ w_rwkv_o[l]
        gm, gr = jnp.split(jax.nn.sigmoid(p_gate), 2, axis=-1)
        x = x + g1 * ((gm * o_mla + gr * o_rwkv) @ w_out[l])
        h = rmsnorm(x, norm_ffn[l]) * (1.0 + sc2) + sh2
        u, z = jnp.split(h @ w_ffn_in[l], 2, axis=-1)
        x = x + g2 * ((jax.nn.silu(u) * z) @ w_ffn_out[l])
    return rmsnorm(x, final_norm)


def setup_inputs(seed: int = 0) -> dict:
    key = jax.random.key(seed)
    ks = jax.random.split(key, 32)
    L = DEPTH

    def nrm(k, shape, scale):
        return jax.random.normal(k, shape, jnp.float32) * scale

    def gain(k, shape):
        return 1.0 + nrm(k, shape, 0.02)

    w0_base = jnp.linspace(-6.0, -1.0, RWKV_WIDTH, dtype=jnp.float32)
    return {
        "x_prompt": nrm(ks[0], (BATCH, SEQ, D_MODEL), 1.0),
        "x_sample": nrm(ks[1], (DEC_BATCH, DEC_SEQ, D_MODEL), 1.0),
        "c_prompt": nrm(ks[2], (BATCH, D_MODEL), 1.0),
        "c_sample": nrm(ks[3], (DEC_BATCH, D_MODEL), 1.0),
        "w_ada": nrm(ks[4], (L, D_MODEL, 6 * D_MODEL), 0.5 * D_MODEL ** -0.5),
        "b_ada": nrm(ks[5], (L, 6 * D_MODEL), 0.01),
        "norm_mix": gain(ks[6], (L, D_MODEL)),
        "w_in": nrm(ks[7], (L, D_MODEL, IN_COLS), D_MODEL ** -0.5),
        "q_a_norm": gain(ks[8], (L, Q_LORA)),
        "kv_a_norm": gain(ks[9], (L, KV_LORA)),
        "w_uq": nrm(ks[10], (L, Q_LORA, MLA_HEADS * (QK_NOPE + QK_ROPE)), Q_LORA ** -0.5),
        "w_ukv": nrm(ks[11], (L, KV_LORA, MLA_HEADS * (QK_NOPE + V_HEAD)), KV_LORA ** -0.5),
        "mu_shift": jax.random.uniform(ks[12], (L, RWKV_IN), jnp.float32, 0.0, 1.0),
        "w0": w0_base[None, None, :] + nrm(ks[13], (L, 2, RWKV_WIDTH), 0.1),
        "w_decay_up": nrm(ks[14], (L, 2, DECAY_LORA, RWKV_WIDTH), 0.1 * DECAY_LORA ** -0.5),
        "a0": nrm(ks[15], (L, 2, RWKV_WIDTH), 0.1),
        "w_iclr_up": nrm(ks[16], (L, 2, ICLR_LORA, RWKV_WIDTH), ICLR_LORA ** -0.5),
        "w_gate_up": nrm(ks[17], (L, GATE_LORA, RWKV_WIDTH), GATE_LORA ** -0.5),
        "k_k": 0.85 + nrm(ks[18], (L, RWKV_WIDTH), 0.05),
        "k_a": 1.0 + nrm(ks[19], (L, RWKV_WIDTH), 0.05),
        "r_k": nrm(ks[20], (L, RWKV_HEADS, RWKV_HEAD), 0.1),
        "lnx_w": gain(ks[21], (L, RWKV_WIDTH)),
        "lnx_b": nrm(ks[22], (L, RWKV_WIDTH), 0.01),
        "w_mla_o": nrm(ks[23], (L, MLA_WIDTH, D_MODEL), MLA_WIDTH ** -0.5),
        "w_rwkv_o": nrm(ks[24], (L, RWKV_WIDTH, D_MODEL), RWKV_WIDTH ** -0.5),
        "w_out": nrm(ks[25], (L, D_MODEL, D_MODEL), D_MODEL ** -0.5),
        "norm_ffn": gain(ks[26], (L, D_MODEL)),
        "w_ffn_in": nrm(ks[27], (L, D_MODEL, 2 * D_FF), D_MODEL ** -0.5),
        "w_ffn_out": nrm(ks[28], (L, D_FF, D_MODEL), D_FF ** -0.5),
        "final_norm": gain(ks[29], (D_MODEL,)),
    }


def reference(x_prompt, x_sample, c_prompt, c_sample, w_ada, b_ada, norm_mix, w_in, q_a_norm,
              kv_a_norm, w_uq, w_ukv, mu_shift, w0, w_decay_up, a0, w_iclr_up, w_gate_up, k_k,
              k_a, r_k, lnx_w, lnx_b, w_mla_o, w_rwkv_o, w_out, norm_ffn, w_ffn_in, w_ffn_out,
              final_norm):
    weights = (w_ada, b_ada, norm_mix, w_in, q_a_norm, kv_a_norm, w_uq, w_ukv, mu_shift, w0,
               w_decay_up, a0, w_iclr_up, w_gate_up, k_k, k_a, r_k, lnx_w, lnx_b, w_mla_o,
               w_rwkv_o, w_out, norm_ffn, w_ffn_in, w_ffn_out, final_norm)
    y_prompt = encoder(x_prompt, c_prompt, *weights)
    y_sample = encoder(x_sample, c_sample, *weights)
    return (y_prompt, y_sample)
```

```python
import numpy as np
import os
ASTOP = int(os.environ.get('ASTOP', 99))
RSTOP = int(os.environ.get('RSTOP', 99))
from contextlib import ExitStack
import concourse.bass as bass
import concourse.mybir as mybir
from concourse.bass_utils import run_bass_kernel_spmd

F32 = mybir.dt.float32
BF16 = mybir.dt.bfloat16
AF = mybir.ActivationFunctionType
ALU = mybir.AluOpType
AX = mybir.AxisListType


class Sched:
    SEM_LIMIT = 30000

    def __init__(self, nc, n_dma_sems=20):
        self.nc = nc
        self.es = ExitStack()
        self.es0 = self.es
        self.engs = {"pe": nc.tensor, "act": nc.scalar, "dve": nc.vector, "pool": nc.gpsimd, "sp": nc.sync}
        self.sem = {}
        self.cnt = {}
        self.waited = {e: {} for e in self.engs}
        self.buf = {}
        self.n_dma_sems = n_dma_sems
        self.dma_sems = {}
        self.dma_rr = {}
        self.nsem = 0
        self.ninst = 0
        self.pe_last = {}

    def __enter__(self):
        self.es.__enter__()
        for e in ("pe", "act", "dve", "pool"):
            self._new_sem(e)
        for q in ("sp", "pool", "act"):
            self.dma_sems[q] = [[self._alloc_sem("dq_%s_%d" % (q, i)), 0] for i in range(self.n_dma_sems)]
            self.dma_rr[q] = 0
        return self

    def __exit__(self, *a):
        return self.es.__exit__(*a)

    def _alloc_sem(self, name):
        self.nsem += 1
        return self.es0.enter_context(self.nc.semaphore("%s_%d" % (name, self.nsem)))

    def _new_sem(self, e):
        self.sem[e] = self._alloc_sem("s_" + e)
        self.cnt[e] = 0

    def sb(self, name, shape, dtype):
        return self.es.enter_context(self.nc.sbuf_tensor("sb_" + name, list(shape), dtype))

    def ps(self, name, i=None):
        return self.es.enter_context(self.nc.psum_tensor(name, [128, 512], F32))

    def _b(self, k):
        b = self.buf.get(k)
        if b is None:
            b = self.buf[k] = {"w": None, "r": {}}
        return b

    def _emit(self, E, fn, reads, writes, dma_q=None, rg=None):
        deps = {}

        def add(d):
            if d is None:
                return
            s, v, owner = d
            if owner == "pe" and E == "pe" and dma_q is None:
                return
            key = id(s)
            if key not in deps or deps[key][1] < v:
                deps[key] = (s, v)

        for k in reads:
            b = self._b(k)
            add(b["w"])
            if k.startswith("bank"):
                for d in b["r"].values():
                    if d[2] != E:
                        add(d)
        for k in writes:
            b = self._b(k)
            add(b["w"])
            for d in b["r"].values():
                add(d)
        if E == "pe" and dma_q is None:
            for k in writes:
                if k.startswith("bank"):
                    last = self.pe_last.get(k)
                    if last is not None and last[0] is not None and rg is not None and last[0] != rg:
                        s_, v_, _o = last[1]
                        if id(s_) not in deps or deps[id(s_)][1] < v_:
                            deps[id(s_)] = (s_, v_)
        eng = self.engs[E]
        slot = None
        if dma_q is not None:
            pool = self.dma_sems[E]
            slot = pool[self.dma_rr[E] % len(pool)]
            self.dma_rr[E] += 1
            if slot[1] > 0:
                add((slot[0], 16 * slot[1], "dma"))
        w = self.waited[E]
        for key, (s, v) in deps.items():
            if w.get(key, 0) >= v:
                continue
            eng.wait_ge(s, v)
            w[key] = v
        inst = fn(eng)
        self.ninst += 1
        if dma_q is not None:
            slot[1] += 1
            inst.then_inc(slot[0], 16)
            me = (slot[0], 16 * slot[1], "dma")
        else:
            if self.cnt[E] >= self.SEM_LIMIT:
                self._new_sem(E)
            self.cnt[E] += 1
            inst.then_inc(self.sem[E], 1)
            me = (self.sem[E], self.cnt[E], E)
        for k in reads:
            self._b(k)["r"][id(me[0])] = me
        for k in writes:
            b = self._b(k)
            b["w"] = me
            b["r"] = {}
            if E == "pe" and dma_q is None and k.startswith("bank"):
                self.pe_last[k] = (rg, me)
        return inst

    def pe(self, fn, reads=(), writes=(), rg=None):
        return self._emit("pe", fn, reads, writes, rg=rg)

    def act(self, fn, reads=(), writes=()):
        return self._emit("act", fn, reads, writes)

    def dve(self, fn, reads=(), writes=()):
        return self._emit("dve", fn, reads, writes)

    def pool(self, fn, reads=(), writes=()):
        return self._emit("pool", fn, reads, writes)

    def on(self, E, fn, reads=(), writes=()):
        return self._emit(E, fn, reads, writes)

    def dma(self, q, out, in_, reads=(), writes=(), **kw):
        return self._emit(q, lambda e: e.dma_start(out=out, in_=in_, **kw), reads, writes, dma_q=q)

    def barrier(self):
        self.finish()
        self.buf = {}
        self.pe_last = {}

    def finish(self):
        allw = {}
        for b in self.buf.values():
            ds = list(b["r"].values())
            if b["w"] is not None:
                ds.append(b["w"])
            for (s, v, o) in ds:
                if id(s) not in allw or allw[id(s)][1] < v:
                    allw[id(s)] = (s, v)
        for E in ("sp", "pool", "act", "dve", "pe"):
            for key, (s, v) in allw.items():
                if self.waited[E].get(key, 0) < v:
                    self.engs[E].wait_ge(s, v)
                    self.waited[E][key] = v


D = 1024
NH = 8
QL, KVL, ROPE = 384, 256, 32
INC = 4512
DFF = 2816
EPS = 1e-6
LNX_EPS = 64e-5
C0 = 0.6065306597126334
ATT_SCALE = 96 ** -0.5
V_NMIX, V_NFFN, V_FN, V_BADA, V_QAN, V_KVAN, V_MU, V_W0, V_A0, V_KK, V_KA, V_OMKA_, V_RK = 0, 8, 16, 24, 72, 75, 77, 91, 99, 107, 111, 115, 115
NV = 119


class Ring:
    def __init__(self, S, name, n, shape, dtype):
        self.t = [S.sb("%s%d" % (name, i), shape, dtype) for i in range(n)]
        self.k = ["%s%d" % (name, i) for i in range(n)]
        self.i = 0

    def next(self):
        j = self.i % len(self.t)
        self.i += 1
        return self.t[j], self.k[j]


class PRing:
    def __init__(self, banks, keys):
        self.t, self.k, self.i = banks, keys, 0

    def next(self):
        j = self.i % len(self.t)
        self.i += 1
        return self.t[j], self.k[j]


def build_nc(T0, T1, phases="0ABRC", dbg=False):
    seqT = [T0, T1, T1]
    NT = T0 + 2 * T1
    soff = [0, T0, T0 + T1]
    TM = max(T0, T1)
    nc = bass.Bass("TRN2", target_bir_lowering=False)
    dr = lambda n, s, dt=F32, kind="ExternalInput": nc.dram_tensor(n, list(s), dt, kind=kind).ap()
    x_p = dr("x_p", [T0, D])
    x_s = dr("x_s", [2 * T1, D])
    xsrc = lambda g0, n: (x_p[g0:g0 + n] if g0 < T0 else x_s[g0 - T0:g0 - T0 + n])
    y_p = dr("y_p", [T0, D], kind="ExternalOutput")
    y_s = dr("y_s", [2 * T1, D], kind="ExternalOutput")
    ydst = lambda g0, n: (y_p[g0:g0 + n] if g0 < T0 else y_s[g0 - T0:g0 - T0 + n])
    cT_d = dr("cT", [128, 8, 4])
    vecs_d = dr("vecs", [128, NV])
    lnwb_d = dr("lnwb", [2, 512])
    rope_d = dr("rope", [2, 128, TM])
    w_ada = dr("w_ada", [D, 6 * D])
    w_in = dr("w_in", [D, INC])
    w_kps = dr("w_kps", [D, 32])
    w_uqn = dr("w_uqn", [QL, 512]); w_uqp = dr("w_uqp", [QL, 256]); w_uqs = dr("w_uqs", [QL, 256])
    w_ukn = dr("w_ukn", [KVL, 512]); w_ukv = dr("w_ukv", [KVL, 512])
    w_dec = dr("w_dec", [2, 64, 512]); w_icl = dr("w_icl", [2, 64, 512]); w_gat = dr("w_gat", [128, 512])
    w_mo = dr("w_mo", [512, D]); w_ro = dr("w_ro", [512, D]); w_out = dr("w_out", [D, D])
    w_fi = dr("w_fi", [D, 2 * DFF]); w_fo = dr("w_fo", [DFF, D])
    it = lambda n, s, dt=BF16: dr(n, s, dt, kind=("ExternalOutput" if dbg else "Internal"))
    b_in = it("b_in", [D, INC + 32])
    b_uqn = it("b_uqn", [QL, 512]); b_uqp = it("b_uqp", [QL, 256]); b_uqr = it("b_uqr", [QL, 256])
    b_ukn = it("b_ukn", [KVL, 512]); b_ukv = it("b_ukv", [KVL, 512])
    b_dec = it("b_dec", [128, 512]); b_icl = it("b_icl", [128, 512]); b_gat = it("b_gat", [128, 512])
    b_mo = it("b_mo", [512, D]); b_ro = it("b_ro", [512, D]); b_out = it("b_out", [D, D])
    b_fi = it("b_fi", [D, 2 * DFF]); b_fo = it("b_fo", [DFF, D])
    qT_d = it("qT_d", [NH, 96, NT]); kT_d = it("kT_d", [NH, 64, NT]); kpe_d = it("kpe_d", [32, NT])
    V_d = it("V_d", [NT // 128, 128, NH * 65])
    NTP = NT + 6
    pR_d = it("pR_d", [1792, NTP])
    gT_d = it("gT_d", [2048, NT])
    atT_d = it("atT_d", [512, NT]); rwT_d = it("rwT_d", [512, NT])
    y0_d = it("y0_d", [NT, 512], F32)
    ppos = lambda s: soff[s] + 2 * s

    S = Sched(nc)
    with S:
        identb = S.sb("identb", [128, 128], BF16)
        identf = S.sb("identf", [128, 128], F32)
        onesb = S.sb("onesb", [128, 128], BF16)
        vecs = S.sb("vecs", [128, NV], F32)
        modv = S.sb("modv", [128, 6, 8, 4], F32)
        banks = [S.ps("bank%d" % i) for i in range(8)]
        bkeys = ["bank%d" % i for i in range(8)]
        PS = PRing(banks, bkeys)
        CONST = ["identb", "identf", "onesb", "vecs", "modv"]

        def ident_build(t, key):
            S.pool(lambda e: e.memset(t[:], 1.0), writes=[key])
            S.pool(lambda e: e.affine_select(t[:], t[:], [[-1, 128]], ALU.is_equal, 0.0, base=0, channel_multiplier=1), reads=[key], writes=[key])
        ident_build(identb, "identb")
        ident_build(identf, "identf")
        S.pool(lambda e: e.memset(onesb[:], 1.0), writes=["onesb"])
        S.dma("sp", vecs[:], vecs_d, writes=["vecs"])

        rr = [0]

        def cast_any(out, in_, reads, writes, scale=None, engines=("act", "dve", "pool")):
            E = engines[rr[0] % len(engines)]
            rr[0] += 1
            if E == "act":
                if scale is None:
                    S.act(lambda e: e.activation(out, in_, AF.Copy), reads=reads, writes=writes)
                else:
                    S.act(lambda e: e.mul(out, in_, scale), reads=reads, writes=writes)
            else:
                if scale is None:
                    S.on(E, lambda e: e.tensor_copy(out, in_), reads=reads, writes=writes)
                else:
                    S.on(E, lambda e: e.tensor_scalar(out, in_, scale, None, ALU.mult), reads=reads, writes=writes)

        if "0" in phases:
            with ExitStack() as sc:
                S.es, old = sc, S.es
                stg = Ring(S, "stg", 3, [128, 2048], F32)
                stb = Ring(S, "stb", 3, [128, 2048], BF16)
                nwr = [0]

                def cast_w(src, dst, R, Cc, neg_cols=None):
                    for r0 in range(0, R, 128):
                        rows = min(128, R - r0)
                        for c0 in range(0, Cc, 2048):
                            cw = min(2048, Cc - c0)
                            a, ak = stg.next()
                            b, bk = stb.next()
                            S.dma("sp", a[0:rows, 0:cw], src[r0:r0 + rows, c0:c0 + cw], writes=[ak])
                            if neg_cols is None:
                                cast_any(b[0:rows, 0:cw], a[0:rows, 0:cw], [ak], [bk])
                            else:
                                av = a[0:rows, 0:cw].rearrange("p (g t) -> p g t", t=32)
                                bv = b[0:rows, 0:cw].rearrange("p (g t) -> p g t", t=32)
                                S.act(lambda e: e.mul(bv[:, :, 0:16], av[:, :, 0:16], -1.0), reads=[ak], writes=[bk])
                                S.dve(lambda e: e.tensor_copy(bv[:, :, 16:32], av[:, :, 16:32]), reads=[ak], writes=[bk])
                            nwr[0] += 1
                            S.dma("pool", dst[r0:r0 + rows, c0:c0 + cw], b[0:rows, 0:cw], reads=[bk], writes=["wscr%d" % nwr[0]])
                cast_w(w_in, b_in[:, 0:INC], D, INC)
                cast_w(w_kps, b_in[:, INC:INC + 32], D, 32, neg_cols=True)
                cast_w(w_uqn, b_uqn, QL, 512); cast_w(w_uqp, b_uqp, QL, 256); cast_w(w_uqs, b_uqr, QL, 256, neg_cols=True)
                cast_w(w_ukn, b_ukn, KVL, 512); cast_w(w_ukv, b_ukv, KVL, 512)
                cast_w(w_dec.rearrange("d k n -> (d k) n"), b_dec, 128, 512)
                cast_w(w_icl.rearrange("d k n -> (d k) n"), b_icl, 128, 512)
                cast_w(w_gat, b_gat, 128, 512)
                cast_w(w_mo, b_mo, 512, D); cast_w(w_ro, b_ro, 512, D); cast_w(w_out, b_out, D, D)
                cast_w(w_fi, b_fi, D, 2 * DFF); cast_w(w_fo, b_fo, DFF, D)
                zt = S.sb("zt", [128, 2], BF16)
                S.pool(lambda e: e.memset(zt[:], 0.0), writes=["zt"])
                for s in range(3):
                    for c in (ppos(s), ppos(s) + seqT[s] + 1):
                        for r0 in range(0, 1792, 128):
                            nwr[0] += 1
                            S.dma("pool", pR_d[r0:r0 + 128, c:c + 1], zt[:, 0:1], reads=["zt"], writes=["wscr%d" % nwr[0]], allow_slow_non_contiguous=True)
                cT = S.sb("cT", [128, 8, 4], F32)
                sT = S.sb("sT", [128, 8, 4], F32)
                modT = S.sb("modT", [128, 48, 4], F32)
                S.dma("sp", cT[:], cT_d, writes=["cT"])
                S.act(lambda e: e.activation(sT[:], cT[:], AF.Silu), reads=["cT"], writes=["sT"])
                wa = Ring(S, "wa", 2, [128, 8, 512], F32)
                w_ada_v = w_ada.rearrange("(k p) n -> p k n", p=128)
                pb, pk = PS.next()
                for piece in range(12):
                    a, ak = wa.next()
                    S.dma("sp", a[:], w_ada_v[:, :, piece * 512:(piece + 1) * 512], writes=[ak])
                    for mm_ in range(4):
                        m = piece * 4 + mm_
                        for k in range(8):
                            S.pe(lambda e: e.matmul(pb[:, m * 4:m * 4 + 4], a[:, k, mm_ * 128:(mm_ + 1) * 128], sT[:, k, :], start=(k == 0), stop=(k == 7)),
                                 reads=[ak, "sT"], writes=[pk])
                S.dve(lambda e: e.tensor_tensor(modT[:], pb[:, 0:192].rearrange("p (m s) -> p m s", s=4),
                                                vecs[:, V_BADA:V_BADA + 48].unsqueeze(2).to_broadcast([128, 48, 4]), ALU.add),
                      reads=[pk, "vecs"], writes=["modT"])
                for (dst, src, nv) in ((0, 8, V_NMIX), (3, 32, V_NFFN)):
                    S.dve(lambda e: e.tensor_scalar(modv[:, dst], modT[:, src:src + 8, :], 1.0, None, ALU.add), reads=["modT"], writes=["modv"])
                    S.dve(lambda e: e.tensor_tensor(modv[:, dst], modv[:, dst], vecs[:, nv:nv + 8].unsqueeze(2).to_broadcast([128, 8, 4]), ALU.mult),
                          reads=["modv", "vecs"], writes=["modv"])
                for (dst, src) in ((1, 0), (2, 16), (4, 24), (5, 40)):
                    S.dve(lambda e: e.tensor_copy(modv[:, dst], modT[:, src:src + 8, :]), reads=["modT"], writes=["modv"])
                S.barrier()
                S.es = old

        if "A" in phases:
            with ExitStack() as sc:
                S.es, old = sc, S.es
                win = S.sb("win", [128, 8, INC + 32], BF16)
                wuqn = S.sb("wuqn", [128, 3, 512], BF16); wuqp = S.sb("wuqp", [128, 3, 256], BF16); wuqr = S.sb("wuqr", [128, 3, 256], BF16)
                wukn = S.sb("wukn", [128, 2, 512], BF16); wukv = S.sb("wukv", [128, 2, 512], BF16)
                WA = ["win", "wuqn", "wuqp", "wuqr", "wukn", "wukv"]
                b_in_v = b_in.rearrange("(k p) n -> p k n", p=128)
                for k in range(8):
                    S.dma("sp", win[:, k, :], b_in_v[:, k, :], writes=["win"])
                for (t, src, key) in ((wuqn, b_uqn, "wuqn"), (wuqp, b_uqp, "wuqp"), (wuqr, b_uqr, "wuqr"), (wukn, b_ukn, "wukn"), (wukv, b_ukv, "wukv")):
                    S.dma("sp", t[:], src.rearrange("(k p) n -> p k n", p=128), writes=[key])
                xt = Ring(S, "xt", 3, [128, D], F32)
                junk = S.sb("junk", [128, D], BF16)
                ssq = Ring(S, "ssq", 2, [128, 8], F32)
                xsbr = Ring(S, "xsb", 2, [128, 4, D], BF16)
                hT = S.sb("hT", [128, 8, 512], BF16)
                cqg = S.sb("cqg", [128, 5, 512], BF16)
                sq = S.sb("sq", [128, 5, 512], BF16)
                rbc = S.sb("rbc", [128, 2, 512], F32)
                rtok = S.sb("rtok", [128, 8], F32)
                cs = Ring(S, "cs", 2, [128, 2, 512], F32)
                csr = S.sb("csr", [128, 2, 512], F32)
                tmpA = Ring(S, "tmpA", 4, [128, 512], F32)
                prw = Ring(S, "prw", 1, [128, 14, 512], BF16)
                gts = Ring(S, "gts", 1, [128, 16, 512], BF16)
                qo = Ring(S, "qo", 1, [128, 6, 512], BF16)
                ko = Ring(S, "ko", 1, [128, 5, 512], BF16)
                vo = Ring(S, "vo", 2, [128, 4, NH, 65], BF16)
                for i in range(len(vo.t)):
                    S.pool(lambda e: e.memset(vo.t[i][:], 1.0), writes=[vo.k[i]])
                nst = [0]
                print("phase A sbuf bytes remaining", nc.sbuf_bytes_remaining)

                def store(dst, src, reads):
                    nst[0] += 1
                    S.dma("sp", dst, src, reads=reads, writes=["ascr%d" % nst[0]])

                chunks = ([("cq", i * 128, 128, i) for i in range(3)] + [("ckv", 384 + i * 128, 128, i) for i in range(2)]
                          + [("kpe", 640, 32, 0), ("kpr", INC, 32, 0)]
                          + [("rw", 672 + i * 128, 128, i) for i in range(14)] + [("gate", 2464 + i * 128, 128, i) for i in range(16)])
                askip = os.environ.get('ASKIP', '').split(',')
                chunks = [c for c in chunks if c[0] not in askip]
                def stage1(s, blk):
                    l0 = blk * 512
                    g0 = soff[s] + l0
                    c_t, c_k = cs.next()
                    S.dma("sp", c_t[:], rope_d[:, :, l0:l0 + 512].rearrange("c p t -> p c t"), writes=[c_k])
                    sq_t, sq_k = ssq.next()
                    xsb, xsbk = xsbr.next()
                    for j in range(4):
                        x_t, x_k = xt.next()
                        S.dma("sp", x_t[:], xsrc(g0 + j * 128, 128), writes=[x_k])
                        S.act(lambda e: e.activation(junk[:], x_t[:], AF.Square, accum_out=sq_t[:, j:j + 1]), reads=[x_k], writes=["junk", sq_k])
                        S.act(lambda e: e.activation(sq_t[:, 4 + j:5 + j], sq_t[:, j:j + 1], AF.Sqrt, bias=EPS, scale=1.0 / D), reads=[sq_k], writes=[sq_k])
                        S.dve(lambda e: e.reciprocal(sq_t[:, 4 + j:5 + j], sq_t[:, 4 + j:5 + j]), reads=[sq_k], writes=[sq_k])
                        S.dve(lambda e: e.tensor_scalar(xsb[:, j, :], x_t[:], sq_t[:, 4 + j:5 + j], None, ALU.mult), reads=[x_k, sq_k], writes=[xsbk + "_%d" % j])
                    return (c_t, c_k, xsb, xsbk)

                blist = [(s, blk) for s in range(3) for blk in range(seqT[s] // 512)]
                nxt = stage1(*blist[0])
                for bidx, (s, blk) in enumerate(blist):
                    if True:
                        l0 = blk * 512
                        g0 = soff[s] + l0
                        c_t, c_k, xsb, xsbk = nxt
                        for k in range(8):
                            pb, pk = PS.next()
                            pbb = pb[:].bitcast(BF16)
                            for j in range(4):
                                S.pe(lambda e: e.transpose(pbb[:, j * 128:(j + 1) * 128], xsb[:, j, k * 128:(k + 1) * 128], identb[:]), reads=[xsbk + "_%d" % j, "identb"], writes=[pk])
                            S.dve(lambda e: e.tensor_scalar(hT[:, k, :], pbb[:, 0:512], modv[:, 0, k, s:s + 1], modv[:, 1, k, s:s + 1], ALU.mult, ALU.add),
                                  reads=[pk, "modv"], writes=["hT%d" % k])
                        if bidx + 1 < len(blist):
                            nxt = stage1(*blist[bidx + 1])
                        p_t, p_k = prw.next()
                        g_t, g_k = gts.next()
                        q_t, q_k = qo.next()
                        k_t, k_k = ko.next()
                        v_t, v_k = vo.next()
                        kpe_ps = {}
                        for (kind, c0, M, i) in chunks:
                            pb, pk = PS.next()
                            for k in range(8):
                                S.pe(lambda e: e.matmul(pb[0:M, :], win[:, k, c0:c0 + M], hT[:, k, :], start=(k == 0), stop=(k == 7)), reads=["win", "hT%d" % k], writes=[pk])
                            AOPS = int(os.environ.get('AOPS', 7))
                            if kind in ("cq", "ckv") and AOPS != 7:
                                ii = i if kind == "cq" else 3 + i
                                vcol = (V_QAN + i) if kind == "cq" else (V_KVAN + i)
                                if AOPS & 2:
                                    S.act(lambda e: e.activation(sq[:, ii, :], pb[:, :], AF.Square), reads=[pk], writes=["sq%d" % ii])
                                if AOPS & 4:
                                    S.dve(lambda e: e.tensor_scalar(cqg[:, ii, :], pb[:, :], vecs[:, vcol:vcol + 1], None, ALU.mult), reads=[pk, "vecs"], writes=["cqg%d" % ii])
                            elif kind in ("cq", "ckv"):
                                ii = i if kind == "cq" else 3 + i
                                vcol = (V_QAN + i) if kind == "cq" else (V_KVAN + i)
                                S.act(lambda e: e.activation(sq[:, ii, :], pb[:, :], AF.Square), reads=[pk], writes=["sq%d" % ii])
                                S.dve(lambda e: e.tensor_scalar(cqg[:, ii, :], pb[:, :], vecs[:, vcol:vcol + 1], None, ALU.mult), reads=[pk, "vecs"], writes=["cqg%d" % ii])
                            elif kind == "kpe":
                                kpe_ps["a"] = (pb, pk)
                            elif kind == "kpr":
                                pa, pak = kpe_ps["a"]
                                t1, t1k = tmpA.next()
                                t2, t2k = tmpA.next()
                                S.dve(lambda e: e.tensor_tensor(t1[0:32, :], pa[0:32, :], c_t[0:32, 0, :], ALU.mult), reads=[pak, c_k], writes=[t1k])
                                S.dve(lambda e: e.tensor_tensor(t2[0:32, :], pb[0:32, :], c_t[0:32, 1, :], ALU.mult), reads=[pk, c_k], writes=[t2k])
                                S.pool(lambda e: e.tensor_tensor(k_t[0:32, 4, :], t1[0:32, :], t2[0:32, :], ALU.add), reads=[t1k, t2k], writes=[k_k])
                                store(kpe_d[:, g0:g0 + 512], k_t[0:32, 4, :], [k_k])
                            elif kind == "rw":
                                cast_any(p_t[:, i, :], pb[:, :], [pk], [p_k], engines=("act", "dve"))
                            else:
                                S.act(lambda e: e.activation(g_t[:, i, :], pb[:, :], AF.Sigmoid), reads=[pk], writes=[g_k])
                        if ASTOP <= 2:
                            continue
                        cp = ppos(s) + 1 + l0
                        store(pR_d[:, cp:cp + 512].rearrange("(i p) t -> p i t", p=128), p_t[:], [p_k])
                        store(gT_d[:, g0:g0 + 512].rearrange("(i p) t -> p i t", p=128), g_t[:], [g_k])
                        if ASTOP <= 3:
                            continue
                        for (which, i0, n, dim) in ((0, 0, 3, QL), (1, 3, 2, KVL)):
                            pb, pk = PS.next()
                            for i in range(n):
                                S.pe(lambda e: e.matmul(pb[:, :], onesb[:], sq[:, i0 + i, :], start=(i == 0), stop=(i == n - 1)), reads=["onesb", "sq%d" % (i0 + i)], writes=[pk])
                            S.act(lambda e: e.activation(rbc[:, which, :], pb[:, :], AF.Sqrt, bias=EPS, scale=1.0 / dim), reads=[pk], writes=["rbc"])
                            S.dve(lambda e: e.reciprocal(rbc[:, which, :], rbc[:, which, :]), reads=["rbc"], writes=["rbc"])
                        pb, pk = PS.next()
                        for j in range(4):
                            for i in range(2):
                                S.pe(lambda e: e.matmul(pb[:, 2 * j:2 * j + 2], sq[:, 3 + i, j * 128:(j + 1) * 128], onesb[:, 0:2], start=(i == 0), stop=(i == 1)),
                                     reads=["sq%d" % (3 + i), "onesb"], writes=[pk])
                        S.act(lambda e: e.activation(rtok[:], pb[:, 0:8], AF.Sqrt, bias=EPS, scale=1.0 / KVL), reads=[pk], writes=["rtok"])
                        S.dve(lambda e: e.reciprocal(rtok[:], rtok[:]), reads=["rtok"], writes=["rtok"])
                        S.pool(lambda e: e.tensor_tensor(csr[:], c_t[:], rbc[:, 0:1, :].to_broadcast([128, 2, 512]), ALU.mult), reads=[c_k, "rbc"], writes=["csr"])
                        if ASTOP <= 4:
                            continue
                        for c in range(4):
                            pb, pk = PS.next()
                            for kc in range(3):
                                S.pe(lambda e: e.matmul(pb[:, :], wuqn[:, kc, c * 128:(c + 1) * 128], cqg[:, kc, :], start=(kc == 0), stop=(kc == 2)), reads=["wuqn", "cqg%d" % kc], writes=[pk])
                            S.dve(lambda e: e.tensor_tensor(q_t[:, c, :], pb[:, :], rbc[:, 0, :], ALU.mult), reads=[pk, "rbc"], writes=[q_k])
                            for i in range(2):
                                store(qT_d[2 * c + i, 0:64, g0:g0 + 512], q_t[64 * i:64 * i + 64, c, :], [q_k])
                        for c in range(2):
                            pa, pak = PS.next()
                            pb, pk = PS.next()
                            for kc in range(3):
                                S.pe(lambda e: e.matmul(pa[:, :], wuqp[:, kc, c * 128:(c + 1) * 128], cqg[:, kc, :], start=(kc == 0), stop=(kc == 2)), reads=["wuqp", "cqg%d" % kc], writes=[pak])
                            for kc in range(3):
                                S.pe(lambda e: e.matmul(pb[:, :], wuqr[:, kc, c * 128:(c + 1) * 128], cqg[:, kc, :], start=(kc == 0), stop=(kc == 2)), reads=["wuqr", "cqg%d" % kc], writes=[pk])
                            t1, t1k = tmpA.next()
                            t2, t2k = tmpA.next()
                            S.dve(lambda e: e.tensor_tensor(t1[:], pa[:, :], csr[:, 0, :], ALU.mult), reads=[pak, "csr"], writes=[t1k])
                            S.dve(lambda e: e.tensor_tensor(t2[:], pb[:, :], csr[:, 1, :], ALU.mult), reads=[pk, "csr"], writes=[t2k])
                            S.pool(lambda e: e.tensor_tensor(q_t[:, 4 + c, :], t1[:], t2[:], ALU.add), reads=[t1k, t2k], writes=[q_k])
                            for i in range(4):
                                store(qT_d[4 * c + i, 64:96, g0:g0 + 512], q_t[32 * i:32 * i + 32, 4 + c, :], [q_k])
                        if ASTOP <= 5:
                            continue
                        for c in range(4):
                            pb, pk = PS.next()
                            for kc in range(2):
                                S.pe(lambda e: e.matmul(pb[:, :], wukn[:, kc, c * 128:(c + 1) * 128], cqg[:, 3 + kc, :], start=(kc == 0), stop=(kc == 1)), reads=["wukn", "cqg%d" % (3 + kc)], writes=[pk])
                            S.dve(lambda e: e.tensor_tensor(k_t[:, c, :], pb[:, :], rbc[:, 1, :], ALU.mult), reads=[pk, "rbc"], writes=[k_k])
                            for i in range(2):
                                store(kT_d[2 * c + i, :, g0:g0 + 512], k_t[64 * i:64 * i + 64, c, :], [k_k])
                        for j in range(4):
                            pb, pk = PS.next()
                            for kc in range(2):
                                S.pe(lambda e: e.matmul(pb[:, :], cqg[:, 3 + kc, j * 128:(j + 1) * 128], wukv[:, kc, :], start=(kc == 0), stop=(kc == 1)), reads=["wukv", "cqg%d" % (3 + kc)], writes=[pk])
                            S.dve(lambda e: e.tensor_scalar(v_t[:, j, :, 0:64], pb[:, :].rearrange("p (h d) -> p h d", d=64), rtok[:, 2 * j:2 * j + 1], None, ALU.mult),
                                  reads=[pk, "rtok"], writes=[v_k])
                        store(V_d[g0 // 128:g0 // 128 + 4].rearrange("n p c -> p n c"), v_t[:].rearrange("p n h d -> p n (h d)"), [v_k])
                S.barrier()
                S.es = old

        if "B" in phases:
            with ExitStack() as sc:
                S.es, old = sc, S.es
                NKT = TM // 128
                Vsb = S.sb("Vsb", [128, NKT, NH * 65], BF16)
                kTr = Ring(S, "kTh", 2, [96, TM], BF16)
                qr = Ring(S, "qh", 3, [96, 512], BF16)
                pTr = Ring(S, "pT", 4, [128, 512], BF16)
                osb = Ring(S, "osb", 2, [65, 512], F32)
                rden = Ring(S, "rden", 2, [64, 512], F32)
                ao = Ring(S, "ao", 2, [64, 512], BF16)
                sel65 = S.sb("sel65", [65, 64], F32)
                S.pool(lambda e: e.memset(sel65[:], 0.0), writes=["sel65"])
                S.pool(lambda e: e.memset(sel65[64:65, :], 1.0), writes=["sel65"])
                PSS = PRing(banks[0:4], bkeys[0:4])
                PO = PRing(banks[4:6], bkeys[4:6])
                PD = PRing(banks[6:8], bkeys[6:8])
                nstB = [0]
                for s in range(3):
                    T = seqT[s]
                    nkt = T // 128
                    for t0 in range(0, nkt, 8):
                        nn = min(8, nkt - t0)
                        S.dma("sp", Vsb[:, t0:t0 + nn, :], V_d[soff[s] // 128 + t0:soff[s] // 128 + t0 + nn].rearrange("n p c -> p n c"), writes=["Vsb%d" % (t0 // 8)])
                    for h in range(NH):
                        kt_t, kt_k = kTr.next()
                        S.dma("sp", kt_t[0:64, 0:T], kT_d[h, :, soff[s]:soff[s] + T], writes=[kt_k])
                        S.dma("sp", kt_t[64:96, 0:T], kpe_d[:, soff[s]:soff[s] + T], writes=[kt_k])
                        for qb in range(T // 512):
                            g0 = soff[s] + qb * 512
                            q_t, q_k = qr.next()
                            S.dma("sp", q_t[:], qT_d[h, :, g0:g0 + 512], writes=[q_k])
                            po, pok = PO.next()
                            LA = 2
                            pend = []
                            for i in range(nkt + LA):
                                if i < nkt:
                                    ps_, psk = PSS.next()
                                    S.pe(lambda e: e.matmul(ps_[:, :], kt_t[:, i * 128:(i + 1) * 128], q_t[:, :], start=True, stop=True), reads=[kt_k, q_k], writes=[psk])
                                    p_t, p_k = pTr.next()
                                    S.act(lambda e: e.activation(p_t[:], ps_[:, :], AF.Exp, scale=ATT_SCALE), reads=[psk], writes=[p_k])
                                    pend.append((i, p_t, p_k))
                                if i >= LA:
                                    (kt, pp_t, pp_k) = pend.pop(0)
                                    S.pe(lambda e: e.matmul(po[0:65, :], Vsb[:, kt, h * 65:(h + 1) * 65], pp_t[:], start=(kt == 0), stop=(kt == nkt - 1)),
                                         reads=["Vsb%d" % (kt // 8), pp_k], writes=[pok])
                            o_t, o_k = osb.next()
                            S.dve(lambda e: e.tensor_copy(o_t[:], po[0:65, :]), reads=[pok], writes=[o_k])
                            pd, pdk = PD.next()
                            S.pe(lambda e: e.matmul(pd[0:64, :], sel65[:, :], o_t[:], start=True, stop=True), reads=["sel65", o_k], writes=[pdk])
                            r_t, r_k = rden.next()
                            S.dve(lambda e: e.reciprocal(r_t[:], pd[0:64, :]), reads=[pdk], writes=[r_k])
                            a_t, a_k = ao.next()
                            S.dve(lambda e: e.tensor_tensor(a_t[:], o_t[0:64, :], r_t[:], ALU.mult), reads=[o_k, r_k], writes=[a_k])
                            nstB[0] += 1
                            S.dma("pool", atT_d[h * 64:(h + 1) * 64, g0:g0 + 512], a_t[:], reads=[a_k], writes=["bscr%d" % nstB[0]])
                S.barrier()
                S.es = old


        if "R" in phases:
            with ExitStack() as sc:
                S.es, old = sc, S.es
                PS4 = PRing(banks[0:5], bkeys[0:5])
                PH = PRing(banks[5:6], bkeys[5:6])
                PY = PRing(banks[6:8], bkeys[6:8])
                def tri(name, pattern, cm, op):
                    t = S.sb(name, [128, 128], BF16)
                    S.pool(lambda e: e.memset(t[:], 1.0), writes=[name])
                    S.pool(lambda e: e.affine_select(t[:], t[:], pattern, op, 0.0, base=0, channel_multiplier=cm), reads=[name], writes=[name])
                    S.pool(lambda e: e.memset(t[0:64, 64:128], 0.0), reads=[name], writes=[name])
                    S.pool(lambda e: e.memset(t[64:128, 0:64], 0.0), reads=[name], writes=[name])
                    return t
                mSU = tri("mSU", [[1, 128]], -1, ALU.is_gt)
                mUI = tri("mUI", [[1, 128]], -1, ALU.is_ge)
                mSL = tri("mSL", [[-1, 128]], 1, ALU.is_gt)
                mLI = tri("mLI", [[-1, 128]], 1, ALU.is_ge)
                M12 = []
                for d_, (ma, mb, na, nb) in enumerate(((mSU, mUI, "mSU", "mUI"), (mSL, mLI, "mSL", "mLI"))):
                    t = S.sb("M12_%d" % d_, [128, 4, 128], BF16)
                    for i_ in range(4):
                        src, sk = ((ma, na), (mb, nb))[i_ % 2]
                        S.pool(lambda e: e.tensor_copy(t[:, i_, :], src[:]), reads=[sk], writes=["M12"])
                    M12.append(t)
                M3 = [mSL, mSU]
                M3k = ["mSL", "mSU"]
                CM = S.sb("CM", [64, 2, 128], BF16)
                S.pool(lambda e: e.memset(CM[:], 0.0), writes=["CM"])
                S.pool(lambda e: e.memset(CM[:, 0, 0:64], 1.0), reads=["CM"], writes=["CM"])
                S.pool(lambda e: e.memset(CM[:, 1, 64:128], 1.0), reads=["CM"], writes=["CM"])
                HIND = S.sb("HIND", [128, 4, 8], BF16)
                S.pool(lambda e: e.memset(HIND[:], 0.0), writes=["HIND"])
                for pr in range(4):
                    for hf in range(2):
                        S.pool(lambda e: e.memset(HIND[64 * hf:64 * hf + 64, pr, 2 * pr + hf:2 * pr + hf + 1], 1.0), reads=["HIND"], writes=["HIND"])
                BD1 = S.sb("BD1", [128, 128], BF16)
                S.pool(lambda e: e.memset(BD1[:], 0.0), writes=["BD1"])
                S.pool(lambda e: e.memset(BD1[0:64, 0:64], 1.0), reads=["BD1"], writes=["BD1"])
                S.pool(lambda e: e.memset(BD1[64:128, 64:128], 1.0), reads=["BD1"], writes=["BD1"])
                msk01 = S.sb("msk01", [128, 8, 64], F32)
                S.pool(lambda e: e.memset(msk01[:], 1.0), writes=["msk01"])
                S.pool(lambda e: e.memset(msk01[:, :, 0:1], 0.0), reads=["msk01"], writes=["msk01"])
                lnwb = S.sb("lnwb", [128, 2, 512], F32)
                for i_ in range(2):
                    S.dma("sp", lnwb[:, i_, :], lnwb_d[i_:i_ + 1, :].to_broadcast([128, 512]), writes=["lnwb"])
                wdec = S.sb("wdec", [64, 2, 512], BF16)
                wicl = S.sb("wicl", [128, 2, 512], BF16)
                wgat = S.sb("wgat", [128, 512], BF16)
                S.dma("sp", wdec[:], b_dec.rearrange("(d k) n -> k d n", d=2), writes=["wdec"])
                S.dma("sp", wicl[64:128], b_icl.rearrange("(d k) n -> k d n", d=2), writes=["wicl"])
                S.dma("sp", wgat[:], b_gat, writes=["wgat"])
                omka = S.sb("omka", [128, 4], F32)
                muh = S.sb("muh", [128, 14], F32); omm = S.sb("omm", [128, 14], F32)
                S.dve(lambda e: e.tensor_scalar(muh[:], vecs[:, V_MU:V_MU + 14], 0.5, None, ALU.mult), reads=["vecs"], writes=["muh"])
                S.dve(lambda e: e.tensor_scalar(omm[:], vecs[:, V_MU:V_MU + 14], -1.0, 1.0, ALU.mult, ALU.add), reads=["vecs"], writes=["omm"])
                S.dve(lambda e: e.tensor_scalar(omka[:], vecs[:, V_KA:V_KA + 4], -1.0, 1.0, ALU.mult, ALU.add), reads=["vecs"], writes=["omka"])
                RC = ["mSU", "mUI", "mSL", "mLI", "M12", "CM", "HIND", "BD1", "msk01", "lnwb", "wdec", "wicl", "wgat", "omka"]
                pin = Ring(S, "pin", 1, [128, 4, 514], BF16)
                xs = S.sb("xs", [128, 8, 512], F32)
                tn = {}
                for nm in ("sg", "E", "E2", "tmp", "Gx", "G", "Gi", "Dd", "aa", "ao_", "kk", "rs", "kkn", "tt", "kd", "kdo", "ka"):
                    tn[nm] = S.sb("r_" + nm, [128, 512], BF16 if nm in ("Gx", "G", "Gi", "Dd", "aa", "ao_", "kkn", "tt", "kd", "kdo", "ka") else F32)
                sqk = S.sb("sqk", [128, 512], BF16)
                twsa = S.sb("twsa", [128, 512], BF16)
                ARr = Ring(S, "AR", 2, [128, 4, 4, 2, 128], BF16)
                BTr = Ring(S, "BT", 2, [128, 4, 512], BF16); KTr = Ring(S, "KT", 2, [128, 4, 512], BF16)
                BHr = Ring(S, "BH", 2, [128, 4, 512], BF16); KHr = Ring(S, "KH", 2, [128, 4, 512], BF16)
                VTr = Ring(S, "VT", 2, [128, 4, 512], BF16)
                RKr = Ring(S, "RK", 2, [128, 4, 512], BF16)
                sgdr = Ring(S, "sgd", 2, [128, 512], BF16)
                GCr = Ring(S, "GC", 2, [64, 8, 8], F32)
                GEnd = S.sb("GEnd", [128, 4, 8], F32)
                ghi = S.sb("ghi", [128, 4, 8], BF16); glo = S.sb("glo", [128, 4, 8], BF16); gdf = S.sb("gdf", [128, 4, 8], F32)
                sb12 = Ring(S, "sb12", 2, [128, 8, 512], BF16)
                Pr = Ring(S, "Pl", 6, [128, 4, 128], BF16)
                Rr = Ring(S, "Rl", 6, [128, 4, 128], BF16)
                TOKr = Ring(S, "TOK", 2, [128, 8, 256], BF16)
                Zr = Ring(S, "Zc", 6, [128, 4, 128], BF16)
                ZFr = Ring(S, "ZF", 2, [128, 8, 128], BF16)
                RTcr = Ring(S, "RTc", 2, [64, 8, 2, 128], BF16)
                Wcr = Ring(S, "Wc", 2, [64, 8, 2, 64], BF16)
                H32 = S.sb("H32", [64, 8, 64], F32)
                Hbf = S.sb("Hbf", [64, 8, 64], BF16)
                tmpH = S.sb("tmpH", [64, 8, 64], F32)
                ytr = Ring(S, "ytile", 3, [128, 512], F32)
                fsq = tn["tmp"]; fyn = S.sb("fyn", [128, 512], F32); fbv = S.sb("fbv", [128, 512], BF16)
                print("phase R sbuf bytes remaining", nc.sbuf_bytes_remaining)
                fst = S.sb("fst", [128, 6, 8], F32)
                fob = S.sb("fob", [128, 512], BF16)
                rwo = Ring(S, "rwo", 2, [128, 4, 128], BF16)
                rwT_v = rwT_d.rearrange("(k p) t -> p k t", p=128)
                nstR = [0]
                v3 = lambda ap: ap.rearrange("p (c t) -> p c t", t=64)
                v4 = lambda ap: ap.rearrange("p (j t) -> p j t", t=128)
                identbc = identb[:].unsqueeze(1).to_broadcast([128, 4, 128])

                def gen_prep(d, s, blk, B):
                    l0 = blk * 512
                    cp = ppos(s) + l0
                    AR, ARk = B["AR"]; BT, BTk = B["BT"]; KT, KTk = B["KT"]; BH, BHk = B["BH"]; KH, KHk = B["KH"]; VT, VTk = B["VT"]
                    GCt, GCk = B["GC"]
                    RK, RKk = B["RK"]
                    sgd, sgdk = B["sgd"]
                    for i in range(14):
                        if i % 4 == 0:
                            p_full, p_k = pin.next()
                            nch = min(4, 14 - i)
                            S.dma("sp", p_full[:, 0:nch, :], pR_d[i * 128:(i + nch) * 128, cp:cp + 514].rearrange("(i p) t -> p i t", p=128), writes=[p_k])
                        ta, tb = (("sg", "E"), ("kk", "rs"))[i % 2]
                        if i < 8:
                            dst, dk = xs[:, i, :], "xs%d" % i
                        elif i < 12:
                            dst, dk = VT[:, i - 8, :], VTk
                        elif i == 12:
                            dst, dk = tn["Gi"][:], "Gi"
                        else:
                            dst, dk = tn["Dd"][:], "Dd"
                        S.pool(lambda e: e.tensor_tensor(tn[ta][:], p_full[:, i % 4, 0:512], p_full[:, i % 4, 2:514], ALU.add), reads=[p_k], writes=[ta])
                        S.dve(lambda e: e.scalar_tensor_tensor(tn[tb][:], tn[ta][:], 0.5, p_full[:, i % 4, 1:513], ALU.mult, ALU.subtract), reads=[ta, p_k], writes=[tb])
                        if i < 8:
                            dst, dk = xs[:, i, :], "xs%d" % i
                        elif i < 12:
                            dst, dk = VT[:, i - 8, :], VTk
                        elif i == 12:
                            dst, dk = tn["Gi"][:], "Gi"
                        else:
                            dst, dk = tn["Dd"][:], "Dd"
                        S.dve(lambda e: e.scalar_tensor_tensor(dst, tn[tb][:], vecs[:, V_MU + i:V_MU + i + 1], p_full[:, i % 4, 1:513], ALU.mult, ALU.add),
                              reads=[tb, "vecs", p_k], writes=[dk])
                        if i % 4 == 3:
                            yield
                    S.act(lambda e: e.activation(twsa[0:64, :], tn["Gi"][0:64, :], AF.Tanh), reads=["Gi"], writes=["twsa"])
                    S.pool(lambda e: e.tensor_copy(twsa[64:128, :], tn["Gi"][64:128, :]), reads=["Gi"], writes=["twsa"])
                    if d == 1:
                        S.act(lambda e: e.activation(sgd[:], tn["Dd"][:], AF.Sigmoid), reads=["Dd"], writes=[sgdk])
                    yield
                    yield
                    yield
                    for pr in range(4):
                        xr, xk = xs[:, pr, :], xs[:, 4 + pr, :]
                        xrk, xkk = "xs%d" % pr, "xs%d" % (4 + pr)
                        pz, pzk = PS4.next()
                        S.pe(lambda e: e.matmul(pz[:, :], wdec[0:64, d, pr * 128:(pr + 1) * 128], twsa[0:64, :], start=True, stop=True), reads=["wdec", "twsa"], writes=[pzk], rg=0)
                        S.act(lambda e: e.activation(tn["sg"][:], pz[:, :], AF.Sigmoid, bias=vecs[:, V_W0 + d * 4 + pr:V_W0 + d * 4 + pr + 1]), reads=[pzk, "vecs"], writes=["sg"])
                        pa_, pak = PS4.next()
                        S.pe(lambda e: e.matmul(pa_[:, :], wicl[64:128, d, pr * 128:(pr + 1) * 128], twsa[64:128, :], start=True, stop=True), reads=["wicl", "twsa"], writes=[pak], rg=64)
                        S.act(lambda e: e.activation(tn["aa"][:], pa_[:, :], AF.Sigmoid, bias=vecs[:, V_A0 + d * 4 + pr:V_A0 + d * 4 + pr + 1]), reads=[pak, "vecs"], writes=["aa"])
                        S.dve(lambda e: e.tensor_scalar(tn["kk"][:], xk, vecs[:, V_KK + pr:V_KK + pr + 1], None, ALU.mult), reads=[xkk, "vecs"], writes=["kk"])
                        S.act(lambda e: e.activation(sqk[:], tn["kk"][:], AF.Square), reads=["kk"], writes=["sqk"])
                        yield
                        yield
                        pq, pqk = PS4.next()
                        S.pe(lambda e: e.matmul(pq[:, :], BD1[:], sqk[:], start=True, stop=True), reads=["BD1", "sqk"], writes=[pqk])
                        S.dve(lambda e: e.tensor_tensor_scan(tn["E"][:], msk01[:].rearrange("p c t -> p (c t)"), tn["sg"][:], 0.0, ALU.mult, ALU.add), reads=["msk01", "sg"], writes=["E"])
                        Ek = "E"
                        if d == 1:
                            S.dve(lambda e: e.tensor_tensor(tn["E2"][:], tn["sg"][:], tn["E"][:], ALU.subtract), reads=["sg", "E"], writes=["E2"])
                            S.dve(lambda e: e.tensor_tensor(v3(tn["E2"][:]), v3(tn["E2"][:]), v3(tn["E"][:])[:, :, 63:64].to_broadcast([128, 8, 64]), ALU.add), reads=["E2", "E"], writes=["E2"])
                            Ek = "E2"
                        Et = tn[Ek]
                        ecol = 63 if d == 0 else 0
                        Etot = v3(Et[:])[:, :, ecol:ecol + 1]
                        S.act(lambda e: e.activation(tn["rs"][:], pq[:, :], AF.Ln, bias=1e-12), reads=[pqk], writes=["rs"])
                        yield
                        S.pool(lambda e: e.tensor_tensor(tn["tmp"][:], Et[:], tn["sg"][:], ALU.subtract), reads=[Ek, "sg"], writes=["tmp"])
                        S.act(lambda e: e.activation(tn["Gx"][:], tn["tmp"][:], AF.Exp, scale=-C0), reads=["tmp"], writes=["Gx"])
                        S.act(lambda e: e.activation(tn["G"][:], Et[:], AF.Exp, scale=-C0), reads=[Ek], writes=["G"])
                        S.act(lambda e: e.activation(tn["Gi"][:], Et[:], AF.Exp, scale=C0), reads=[Ek], writes=["Gi"])
                        S.pool(lambda e: e.tensor_tensor(v3(tn["tmp"][:]), Etot.to_broadcast([128, 8, 64]), v3(Et[:]), ALU.subtract), reads=[Ek, "tmp"], writes=["tmp"])
                        S.act(lambda e: e.activation(tn["Dd"][:], tn["tmp"][:], AF.Exp, scale=-C0), reads=["tmp"], writes=["Dd"])
                        S.act(lambda e: e.activation(GEnd[:, pr, :].unsqueeze(2), Etot, AF.Exp, scale=-C0), reads=[Ek], writes=["GEnd"])
                        S.act(lambda e: e.activation(tn["rs"][:], tn["rs"][:], AF.Exp, scale=-0.5), reads=["rs"], writes=["rs"])
                        S.pool(lambda e: e.tensor_tensor(tn["kkn"][:], tn["kk"][:], tn["rs"][:], ALU.mult), reads=["kk", "rs"], writes=["kkn"])
                        S.dve(lambda e: e.tensor_scalar(tn["tt"][:], tn["aa"][:], vecs[:, V_KA + pr:V_KA + pr + 1], omka[:, pr:pr + 1], ALU.mult, ALU.add), reads=["aa", "vecs", "omka"], writes=["tt"])
                        S.pool(lambda e: e.tensor_tensor(tn["kd"][:], xk, tn["tt"][:], ALU.mult), reads=[xkk, "tt"], writes=["kd"])
                        S.pool(lambda e: e.tensor_tensor(tn["ka"][:], tn["kkn"][:], tn["aa"][:], ALU.mult), reads=["kkn", "aa"], writes=["ka"])
                        yield
                        S.dve(lambda e: e.scalar_tensor_tensor(AR[:, pr, :, 0, :], v4(tn["kkn"][:]), -1.0, v4(tn["Gx"][:]), ALU.mult, ALU.mult), reads=["kkn", "Gx"], writes=[ARk])
                        S.dve(lambda e: e.tensor_tensor(AR[:, pr, :, 1, :], v4(xr), v4(tn["G"][:]), ALU.mult), reads=[xrk, "G"], writes=[ARk])
                        S.dve(lambda e: e.tensor_tensor(BT[:, pr, :], tn["ka"][:], tn["Gi"][:], ALU.mult), reads=["ka", "Gi"], writes=[BTk])
                        S.pool(lambda e: e.tensor_tensor(KT[:, pr, :], tn["kd"][:], tn["Gi"][:], ALU.mult), reads=["kd", "Gi"], writes=[KTk])
                        S.dve(lambda e: e.tensor_tensor(BH[:, pr, :], tn["ka"][:], tn["Dd"][:], ALU.mult), reads=["ka", "Dd"], writes=[BHk])
                        S.pool(lambda e: e.tensor_tensor(KH[:, pr, :], tn["kd"][:], tn["Dd"][:], ALU.mult), reads=["kd", "Dd"], writes=[KHk])
                        if d == 1:
                            pa2, pa2k = PS4.next()
                            S.pe(lambda e: e.matmul(pa2[:, :], wicl[64:128, 0, pr * 128:(pr + 1) * 128], twsa[64:128, :], start=True, stop=True), reads=["wicl", "twsa"], writes=[pa2k], rg=64)
                            S.act(lambda e: e.activation(tn["ao_"][:], pa2[:, :], AF.Sigmoid, bias=vecs[:, V_A0 + pr:V_A0 + pr + 1]), reads=[pa2k, "vecs"], writes=["ao_"])
                            S.dve(lambda e: e.tensor_scalar(tn["tt"][:], tn["ao_"][:], vecs[:, V_KA + pr:V_KA + pr + 1], omka[:, pr:pr + 1], ALU.mult, ALU.add), reads=["ao_", "vecs", "omka"], writes=["tt"])
                            S.pool(lambda e: e.tensor_tensor(tn["kdo"][:], xk, tn["tt"][:], ALU.mult), reads=[xkk, "tt"], writes=["kdo"])
                            S.pool(lambda e: e.tensor_tensor(tn["kdo"][:], tn["kdo"][:], tn["kd"][:], ALU.add), reads=["kdo", "kd"], writes=["kdo"])
                            S.dve(lambda e: e.scalar_tensor_tensor(RK[:, pr, :], xr, vecs[:, V_RK + pr:V_RK + pr + 1], tn["kdo"][:], ALU.mult, ALU.mult), reads=[xrk, "vecs", "kdo"], writes=[RKk])
                        yield
                    S.dve(lambda e: e.tensor_copy(ghi[:], GEnd[:]), reads=["GEnd"], writes=["ghi"])
                    S.dve(lambda e: e.tensor_tensor(gdf[:], GEnd[:], ghi[:], ALU.subtract), reads=["GEnd", "ghi"], writes=["gdf"])
                    S.dve(lambda e: e.tensor_copy(glo[:], gdf[:]), reads=["gdf"], writes=["glo"])
                    yield
                    yield
                    pg_, pgk = PS4.next()
                    for h in (0, 2, 4, 6, 1, 3, 5, 7):
                        sl = slice(64 * (h % 2), 64 * (h % 2) + 64)
                        S.pe(lambda e: e.matmul(pg_[0:64, h * 8:(h + 1) * 8], identb[sl, sl], ghi[sl, h // 2, :], start=True, stop=False), reads=["identb", "ghi"], writes=[pgk], rg=64 * (h % 2))
                        S.pe(lambda e: e.matmul(pg_[0:64, h * 8:(h + 1) * 8], identb[sl, sl], glo[sl, h // 2, :], start=False, stop=True), reads=["identb", "glo"], writes=[pgk], rg=64 * (h % 2))
                    S.act(lambda e: e.activation(GCt[:].rearrange("p h c -> p (h c)"), pg_[0:64, 0:64], AF.Copy), reads=[pgk], writes=[GCk])
                    yield

                def gen_group(d, X, q):
                    j, tsl = X["j"], X["tsl"]
                    B = X["B"]
                    AR, ARk = B["AR"]; BT, BTk = B["BT"]; KT, KTk = B["KT"]; BH, BHk = B["BH"]; KH, KHk = B["KH"]; VT, VTk = B["VT"]
                    s12, s12k = X["s12"]; TOK, TOKk = X["TOK"]; ZF, ZFk = X["ZF"]; RTc, RTck = X["RTc"]; Wc, Wck = X["Wc"]; psY, psYk = X["psY"]
                    hs = [4 * q + u for u in range(4)]
                    prs = [h // 2 for h in hs]
                    sls = [slice(64 * (h % 2), 64 * (h % 2) + 64) for h in hs]
                    for u, h in enumerate(hs):
                        b1, b1k = PS4.next()
                        arf = AR[sls[u], prs[u], j].rearrange("p a t -> p (a t)")
                        S.pe(lambda e: e.matmul(b1[:, 0:256], BT[sls[u], prs[u], tsl], arf, start=True, stop=True), reads=[BTk, ARk], writes=[b1k], rg=64 * (h % 2))
                        S.pe(lambda e: e.matmul(b1[:, 256:512], KT[sls[u], prs[u], tsl], arf, start=True, stop=True), reads=[KTk, ARk], writes=[b1k], rg=64 * (h % 2))
                        S.dve(lambda e: e.tensor_tensor(s12[:, h, :], b1[:, :], M12[d][:].rearrange("p a t -> p (a t)"), ALU.mult), reads=[b1k, "M12"], writes=[s12k])
                        if u == 1:
                            yield
                    b3, b3k = PS4.next()
                    for u in (0, 2, 1, 3):
                        h = hs[u]
                        S.pe(lambda e: e.matmul(b3[:, u * 128:(u + 1) * 128], AR[sls[u], prs[u], j, 0, :], BT[sls[u], prs[u], tsl], start=True, stop=True), reads=[ARk, BTk], writes=[b3k], rg=64 * (h % 2))
                    Rc, Rck = Rr.next()
                    S.dve(lambda e: e.tensor_tensor(Rc[:], v4(b3[:, :]), M3[d][:].unsqueeze(1).to_broadcast([128, 4, 128]), ALU.mult), reads=[b3k, M3k[d]], writes=[Rck])
                    bT, bTk = PS4.next()
                    bTb = bT[:].bitcast(BF16)
                    for u in (0, 2, 1, 3):
                        h = hs[u]
                        srcs = ((AR[sls[u], prs[u], j, 0, :], ARk), (VT[sls[u], prs[u], tsl], VTk), (BH[sls[u], prs[u], tsl], BHk), (KH[sls[u], prs[u], tsl], KHk))
                        for i_, (src, sk) in enumerate(srcs):
                            S.pe(lambda e: e.transpose(bTb[:, u * 256 + i_ * 64:u * 256 + i_ * 64 + 64], src, identb[sls[u], sls[u]]), reads=[sk, "identb"], writes=[bTk], rg=64 * (h % 2))
                    S.act(lambda e: e.activation(TOK[:, 4 * q:4 * q + 4, :].rearrange("p u c -> p (u c)"), bTb[:, 0:1024], AF.Copy), reads=[bTk], writes=[TOKk])
                    yield
                    Pc, Pck = s12[:, 4 * q:4 * q + 4, 0:128], s12k
                    Zc, Zck = Zr.next()
                    S.pool(lambda e: e.tensor_copy(Zc[:, :, 0:64], TOK[:, 4 * q:4 * q + 4, 0:64]), reads=[TOKk], writes=[Zck])
                    bA, bAk = PS4.next()
                    for u, h in enumerate(hs):
                        S.pe(lambda e: e.matmul(bA[:, u * 64:(u + 1) * 64], s12[:, h, 256:384], TOK[:, h, 64:128], start=True, stop=True), reads=[s12k, TOKk], writes=[bAk])
                    S.act(lambda e: e.activation(Zc[:, :, 64:128], bA[:, 0:256].rearrange("p (u c) -> p u c", c=64), AF.Copy), reads=[bAk], writes=[Zck])
                    yield
                    for l in range(6):
                        if l < 5:
                            bP, bPk = PS4.next()
                            for u in range(4):
                                S.pe(lambda e: e.matmul(bP[:, u * 128:(u + 1) * 128], Rc[:, u, :], Pc[:, u, :], start=True, stop=True), reads=[Rck, Pck], writes=[bPk])
                            if l < 4:
                                bR, bRk = PS4.next()
                                for u in range(4):
                                    S.pe(lambda e: e.matmul(bR[:, u * 128:(u + 1) * 128], Pc[:, u, :], Rc[:, u, :], start=True, stop=True), reads=[Rck, Pck], writes=[bRk])
                        bZ, bZk = PS4.next()
                        S.pe(lambda e: e.matmul(bZ[:, :], identb[:], Zc[:].rearrange("p u c -> p (u c)"), start=True, stop=False), reads=["identb", Zck], writes=[bZk])
                        for u in range(4):
                            S.pe(lambda e: e.matmul(bZ[:, u * 128:(u + 1) * 128], Pc[:, u, :], Zc[:, u, :], start=False, stop=True), reads=[Pck, Zck], writes=[bZk])
                        Pn = None
                        if l < 5:
                            Pn, Pnk = Pr.next()
                            S.act(lambda e: e.activation(Pn[:], v4(bP[:, :]), AF.Copy), reads=[bPk], writes=[Pnk])
                            if l < 4:
                                Rn, Rnk = Rr.next()
                                S.act(lambda e: e.activation(Rn[:], v4(bR[:, :]), AF.Copy), reads=[bRk], writes=[Rnk])
                                Rc, Rck = Rn, Rnk
                            Zn, Znk = Zr.next()
                            if l % 2 == 0:
                                S.dve(lambda e: e.tensor_copy(Zn[:], v4(bZ[:, :])), reads=[bZk], writes=[Znk])
                            else:
                                S.act(lambda e: e.activation(Zn[:], v4(bZ[:, :]), AF.Copy), reads=[bZk], writes=[Znk])
                            Zc, Zck = Zn, Znk
                            Pc, Pck = Pn[:], Pnk
                        else:
                            S.dve(lambda e: e.tensor_copy(ZF[:, 4 * q:4 * q + 4, :], v4(bZ[:, :])), reads=[bZk], writes=[ZFk])
                        yield
                    bRT, bRTk = PS4.next()
                    for u, h in enumerate(hs):
                        S.pe(lambda e: e.matmul(bRT[0:64, u * 128:(u + 1) * 128], identb[sls[u], sls[u]], AR[sls[u], prs[u], j, 1, :], start=True, stop=False), reads=["identb", ARk], writes=[bRTk], rg=64 * (h % 2))
                        S.pe(lambda e: e.matmul(bRT[0:64, u * 128:(u + 1) * 128], ZF[:, h, 0:64], s12[:, h, 128:256], start=False, stop=True), reads=[ZFk, s12k], writes=[bRTk])
                    S.dve(lambda e: e.tensor_tensor(RTc[:, 4 * q:4 * q + 4, :, :], v4(bRT[0:64, :]).unsqueeze(2).to_broadcast([64, 4, 2, 128]),
                                                    CM[:].unsqueeze(1).to_broadcast([64, 4, 2, 128]), ALU.mult), reads=[bRTk, "CM"], writes=[RTck])
                    bW, bWk = PS4.next()
                    for c in range(2):
                        for u, h in enumerate(hs):
                            S.pe(lambda e: e.matmul(bW[0:64, (u * 2 + c) * 64:(u * 2 + c + 1) * 64], ZF[64 * c:64 * c + 64, h, 0:64], TOK[64 * c:64 * c + 64, h, 128:192], start=True, stop=True),
                                 reads=[ZFk, TOKk], writes=[bWk], rg=64 * c)
                    S.act(lambda e: e.activation(Wc[:, 4 * q:4 * q + 4, :, :].rearrange("p u c k -> p (u c k)"), bW[0:64, :], AF.Copy), reads=[bWk], writes=[Wck])
                    for u, h in enumerate(hs):
                        S.pe(lambda e: e.matmul(psY[:, h * 64:(h + 1) * 64], s12[:, h, 128:256], ZF[:, h, 64:128], start=X["firstY"], stop=False, skip_group_check=True), reads=[s12k, ZFk], writes=[psYk])
                        X["firstY"] = False
                        S.pe(lambda e: e.matmul(psY[:, h * 64:(h + 1) * 64], s12[:, h, 384:512], TOK[:, h, 64:128], start=False, stop=False, skip_group_check=True), reads=[s12k, TOKk], writes=[psYk])
                    yield

                def gen_chain(d, X, B):
                    j, tsl, g0 = X["j"], X["tsl"], X["g0"]
                    TOK, TOKk = X["TOK"]; ZF, ZFk = X["ZF"]; RTc, RTck = X["RTc"]; Wc, Wck = X["Wc"]; psY, psYk = X["psY"]
                    GCt, GCk = B["GC"]
                    RK, RKk = B["RK"]
                    sgd, sgdk = B["sgd"]
                    corder = (0, 1) if d == 0 else (1, 0)
                    for ci, c in enumerate(corder):
                        cg = j * 2 + c
                        for h in range(8):
                            S.pe(lambda e: e.matmul(psY[:, h * 64:(h + 1) * 64], RTc[:, h, c, :], Hbf[:, h, :], start=False, stop=(ci == 1), skip_group_check=True), reads=[RTck, "Hbf"], writes=[psYk], rg=0)
                        bH, bHk = PH.next()
                        csl = slice(64 * c, 64 * c + 64)
                        for h in range(8):
                            S.pe(lambda e: e.matmul(bH[0:64, h * 64:(h + 1) * 64], Wc[:, h, c, :], Hbf[:, h, :], start=(h == 0), stop=False, skip_group_check=True), reads=[Wck, "Hbf"], writes=[bHk], rg=0)
                        for h in range(8):
                            S.pe(lambda e: e.matmul(bH[0:64, h * 64:(h + 1) * 64], TOK[csl, h, 128:192], ZF[csl, h, 64:128], start=False, stop=False, skip_group_check=True), reads=[TOKk, ZFk], writes=[bHk], rg=64 * c)
                            S.pe(lambda e: e.matmul(bH[0:64, h * 64:(h + 1) * 64], TOK[csl, h, 192:256], TOK[csl, h, 64:128], start=False, stop=True, skip_group_check=True), reads=[TOKk], writes=[bHk], rg=64 * c)
                        S.dve(lambda e: e.tensor_tensor(tmpH[:], H32[:], GCt[:, :, cg:cg + 1].to_broadcast([64, 8, 64]), ALU.mult), reads=["H32", GCk], writes=["tmpH"])
                        yield
                        S.dve(lambda e: e.tensor_tensor(H32[:], tmpH[:], bH[0:64, :].rearrange("p (h v) -> p h v", v=64), ALU.add), reads=["tmpH", bHk], writes=["H32"])
                        S.act(lambda e: e.activation(Hbf[:], H32[:], AF.Copy), reads=["H32"], writes=["Hbf"])
                        yield
                    if d == 0:
                        y_t, y_k = ytr.next()
                        S.act(lambda e: e.activation(y_t[:], psY[:, :], AF.Copy), reads=[psYk], writes=[y_k])
                        nstR[0] += 1
                        S.dma("pool", y0_d[g0 + j * 128:g0 + (j + 1) * 128, :], y_t[:], reads=[y_k], writes=["y0scr%d" % nstR[0]])
                    else:
                        y0_t, y0_k = X["y0"]
                        y_t, y_k = ytr.next()
                        S.dve(lambda e: e.tensor_tensor(y_t[:], psY[:, :], y0_t[:], ALU.add), reads=[psYk, y0_k], writes=[y_k])
                        yv = y_t[:].rearrange("p (h v) -> p h v", v=64)
                        S.dve(lambda e: e.tensor_reduce(fst[:, 0, :], yv, AX.X, ALU.add), reads=[y_k], writes=["fst"])
                        S.act(lambda e: e.activation(fsq[:], y_t[:], AF.Square), reads=[y_k], writes=["tmp"])
                        S.dve(lambda e: e.tensor_reduce(fst[:, 1, :], fsq[:].rearrange("p (h v) -> p h v", v=64), AX.X, ALU.add), reads=["tmp"], writes=["fst"])
                        S.dve(lambda e: e.tensor_scalar(fst[:, 2, :], fst[:, 0, :], 1.0 / 64, None, ALU.mult), reads=["fst"], writes=["fst"])
                        S.dve(lambda e: e.tensor_tensor(fst[:, 3, :], fst[:, 2, :], fst[:, 2, :], ALU.mult), reads=["fst"], writes=["fst"])
                        S.dve(lambda e: e.scalar_tensor_tensor(fst[:, 4, :], fst[:, 1, :], 1.0 / 64, fst[:, 3, :], ALU.mult, ALU.subtract), reads=["fst"], writes=["fst"])
                        S.act(lambda e: e.activation(fst[:, 4, :], fst[:, 4, :], AF.Sqrt, bias=LNX_EPS), reads=["fst"], writes=["fst"])
                        S.dve(lambda e: e.reciprocal(fst[:, 4, :], fst[:, 4, :]), reads=["fst"], writes=["fst"])
                        yield
                        fynv = fyn[:].rearrange("p (h v) -> p h v", v=64)
                        S.dve(lambda e: e.tensor_tensor(fynv, yv, fst[:, 2, :].unsqueeze(2).to_broadcast([128, 8, 64]), ALU.subtract), reads=[y_k, "fst"], writes=["fyn"])
                        S.dve(lambda e: e.tensor_tensor(fynv, fynv, fst[:, 4, :].unsqueeze(2).to_broadcast([128, 8, 64]), ALU.mult), reads=["fyn", "fst"], writes=["fyn"])
                        S.pool(lambda e: e.tensor_tensor(fyn[:], fyn[:], lnwb[:, 0, :], ALU.mult), reads=["fyn", "lnwb"], writes=["fyn"])
                        S.pool(lambda e: e.tensor_tensor(fyn[:], fyn[:], lnwb[:, 1, :], ALU.add), reads=["fyn", "lnwb"], writes=["fyn"])
                        pbn, pbnk = PS4.next()
                        for pr in range(4):
                            S.pe(lambda e: e.matmul(pbn[:, 0:8], RK[:, pr, tsl], HIND[:, pr, :], start=(pr == 0), stop=(pr == 3)), reads=[RKk, "HIND"], writes=[pbnk])
                        S.act(lambda e: e.activation(fst[:, 5, :], pbn[:, 0:8], AF.Copy), reads=[pbnk], writes=["fst"])
                        S.dve(lambda e: e.tensor_tensor(fbv[:].rearrange("p (h v) -> p h v", v=64), TOK[:, :, 64:128], fst[:, 5, :].unsqueeze(2).to_broadcast([128, 8, 64]), ALU.mult), reads=[TOKk, "fst"], writes=["fbv"])
                        S.pool(lambda e: e.tensor_tensor(fyn[:], fyn[:], fbv[:], ALU.add), reads=["fyn", "fbv"], writes=["fyn"])
                        yield
                        pgt, pgtk = PS4.next()
                        S.pe(lambda e: e.matmul(pgt[:, :], sgd[:, tsl], wgat[:], start=True, stop=True), reads=[sgdk, "wgat"], writes=[pgtk])
                        S.dve(lambda e: e.tensor_tensor(fob[:], fyn[:], pgt[:, :], ALU.mult), reads=["fyn", pgtk], writes=["fob"])
                        bT2, bT2k = PS4.next()
                        bT2b = bT2[:].bitcast(BF16)
                        for kk_ in range(4):
                            S.pe(lambda e: e.transpose(bT2b[:, kk_ * 128:(kk_ + 1) * 128], fob[:, kk_ * 128:(kk_ + 1) * 128], identb[:]), reads=["fob", "identb"], writes=[bT2k])
                        ro_t, ro_k = rwo.next()
                        S.act(lambda e: e.activation(ro_t[:].rearrange("p k t -> p (k t)"), bT2b[:, 0:512], AF.Copy), reads=[bT2k], writes=[ro_k])
                        nstR[0] += 1
                        S.dma("pool", rwT_v[:, :, g0 + j * 128:g0 + (j + 1) * 128], ro_t[:], reads=[ro_k], writes=["rwscr%d" % nstR[0]])
                    yield

                def run_rr(gens, bg=None, chain=None, chain_delay=3, bg_hold=False):
                    gens = [g_ for g_ in gens if g_ is not None]
                    rnd = 0
                    while gens or chain is not None:
                        for g_ in list(gens):
                            try:
                                next(g_)
                            except StopIteration:
                                gens.remove(g_)
                        if chain is not None and (rnd >= chain_delay or not gens):
                            try:
                                next(chain)
                            except StopIteration:
                                chain = None
                        if bg is not None and bg[0] is not None and (chain is None or not bg_hold):
                            try:
                                next(bg[0])
                            except StopIteration:
                                bg[0] = None
                        rnd += 1

                def new_block():
                    return {"AR": ARr.next(), "BT": BTr.next(), "KT": KTr.next(), "BH": BHr.next(), "KH": KHr.next(), "VT": VTr.next(),
                            "GC": GCr.next(), "RK": RKr.next(), "sgd": sgdr.next()}

                for d in range(2):
                    for s in range(3):
                        T = seqT[s]
                        nblk = T // 512
                        S.pool(lambda e: e.memset(H32[:], 0.0), reads=["H32"], writes=["H32"])
                        S.pool(lambda e: e.memset(Hbf[:], 0.0), reads=["Hbf"], writes=["Hbf"])
                        prev_chain = None
                        order = [bi if d == 0 else nblk - 1 - bi for bi in range(nblk)]
                        Bcur = new_block()
                        run_rr([gen_prep(d, s, order[0], Bcur)])
                        for bi in range(nblk):
                            blk = order[bi]
                            g0 = soff[s] + blk * 512
                            B = Bcur
                            bg = [None]
                            if bi + 1 < nblk:
                                Bcur = new_block()
                                bg[0] = gen_prep(d, s, order[bi + 1], Bcur)
                            for ji in range(4):
                                j = ji if d == 0 else 3 - ji
                                X = {"j": j, "tsl": slice(j * 128, (j + 1) * 128), "g0": g0, "firstY": True, "B": B,
                                     "s12": sb12.next(), "TOK": TOKr.next(), "ZF": ZFr.next(), "RTc": RTcr.next(), "Wc": Wcr.next(), "psY": PY.next()}
                                if d == 1:
                                    X["y0"] = ytr.next()
                                    S.dma("sp", X["y0"][0][:], y0_d[g0 + j * 128:g0 + (j + 1) * 128, :], writes=[X["y0"][1]])
                                run_rr([gen_group(d, X, 0), gen_group(d, X, 1)], bg=bg, chain=prev_chain, bg_hold=(ji == 0))
                                prev_chain = gen_chain(d, X, B)
                            if bg[0] is not None:
                                run_rr([bg[0]])
                        run_rr([prev_chain])
                    S.barrier()
                S.es = old

        if "C" in phases:
            use_rw = "R" in phases
            with ExitStack() as sc:
                S.es, old = sc, S.es
                NB = 256
                wmo = S.sb("wmo", [128, 4, D], BF16); wro = S.sb("wro", [128, 4, D], BF16)
                wout = S.sb("wout", [128, 8, D], BF16); wfi = S.sb("wfi", [128, 8, 2 * DFF], BF16)
                S.dma("sp", wmo[:], b_mo.rearrange("(k p) n -> p k n", p=128), writes=["wmo"])
                S.dma("sp", wro[:], b_ro.rearrange("(k p) n -> p k n", p=128), writes=["wro"])
                S.dma("sp", wout[:], b_out.rearrange("(k p) n -> p k n", p=128), writes=["wout"])
                b_fi_v = b_fi.rearrange("(k p) n -> p k n", p=128)
                for k in range(8):
                    S.dma("sp", wfi[:, k, :], b_fi_v[:, k, :], writes=["wfi"])
                b_fo_v = b_fo.rearrange("(k p) n -> p k n", p=128)
                wfo = Ring(S, "wfo", 2, [128, 22, 128], BF16)
                xtok = Ring(S, "xtok", 2, [128, D], F32)
                xT = S.sb("xT", [128, 8, NB], F32)
                att = Ring(S, "att", 2, [128, 4, NB], BF16)
                rwt = Ring(S, "rwt", 2, [128, 4, NB], BF16)
                gat = Ring(S, "gat", 1, [128, 16, NB], BF16)
                tmpC = Ring(S, "tmpC", 4, [128, NB], F32)
                mix = S.sb("mix", [128, 8, NB], BF16)
                sqc = S.sb("sqc", [128, 8, NB], BF16)
                rb2 = Ring(S, "rb2", 2, [128, NB], F32)
                h2 = S.sb("h2", [128, 8, NB], BF16)
                aff = S.sb("aff", [128, 22, NB], BF16)
                ytok = Ring(S, "ytok", 2, [128, D], F32)
                atT_v = atT_d.rearrange("(k p) t -> p k t", p=128)
                rwT_v = rwT_d.rearrange("(k p) t -> p k t", p=128)
                gT_v = gT_d.rearrange("(k p) t -> p k t", p=128)
                alt = [0]

                def rms_bc(r_t, r_k):
                    for k in range(8):
                        S.act(lambda e: e.activation(sqc[:, k, :], xT[:, k, :], AF.Square), reads=["xT%d" % k], writes=["sqc%d" % k])
                    pa, pak = PS.next()
                    for k in range(8):
                        S.pe(lambda e: e.matmul(pa[:, 0:NB], onesb[:], sqc[:, k, :], start=(k == 0), stop=(k == 7)), reads=["onesb", "sqc%d" % k], writes=[pak])
                    S.act(lambda e: e.activation(r_t[:], pa[:, 0:NB], AF.Sqrt, bias=EPS, scale=1.0 / D), reads=[pak], writes=[r_k])
                    S.dve(lambda e: e.reciprocal(r_t[:], r_t[:]), reads=[r_k], writes=[r_k])

                nyo = [0]
                for s in range(3):
                    for blk in range(seqT[s] // NB):
                        g0 = soff[s] + blk * NB
                        a_t, a_k = att.next()
                        S.dma("sp", a_t[:], atT_v[:, :, g0:g0 + NB], writes=[a_k])
                        w_t, w_k = rwt.next()
                        if use_rw:
                            S.dma("sp", w_t[:], rwT_v[:, :, g0:g0 + NB], writes=[w_k])
                        ga_t, ga_k = gat.next()
                        S.dma("sp", ga_t[:], gT_v[:, :, g0:g0 + NB], writes=[ga_k])
                        for j in range(NB // 128):
                            x_t, x_k = xtok.next()
                            S.dma("sp", x_t[:], xsrc(g0 + j * 128, 128), writes=[x_k])
                            for kq in range(2):
                                pb, pk = PS.next()
                                for kk in range(4):
                                    k = kq * 4 + kk
                                    S.pe(lambda e: e.transpose(pb[:, kk * 128:(kk + 1) * 128], x_t[:, k * 128:(k + 1) * 128], identf[:]), reads=[x_k, "identf"], writes=[pk])
                                cast_any(xT[:, kq * 4:kq * 4 + 4, j * 128:(j + 1) * 128], pb[:, :].rearrange("p (k t) -> p k t", t=128), [pk], ["xT%d" % k_ for k_ in range(kq * 4, kq * 4 + 4)], engines=("act", "dve"))
                        for m in range(8):
                            pa, pak = PS.next()
                            for kc in range(4):
                                S.pe(lambda e: e.matmul(pa[:, 0:NB], wmo[:, kc, m * 128:(m + 1) * 128], a_t[:, kc, :], start=(kc == 0), stop=(kc == 3)), reads=["wmo", a_k], writes=[pak])
                            t1, t1k = tmpC.next()
                            S.dve(lambda e: e.tensor_tensor(t1[:], pa[:, 0:NB], ga_t[:, m, :], ALU.mult), reads=[pak, ga_k], writes=[t1k])
                            if use_rw:
                                pb, pk = PS.next()
                                for kc in range(4):
                                    S.pe(lambda e: e.matmul(pb[:, 0:NB], wro[:, kc, m * 128:(m + 1) * 128], w_t[:, kc, :], start=(kc == 0), stop=(kc == 3)), reads=["wro", w_k], writes=[pk])
                                t2, t2k = tmpC.next()
                                S.dve(lambda e: e.tensor_tensor(t2[:], pb[:, 0:NB], ga_t[:, 8 + m, :], ALU.mult), reads=[pk, ga_k], writes=[t2k])
                                S.pool(lambda e: e.tensor_tensor(mix[:, m, :], t1[:], t2[:], ALU.add), reads=[t1k, t2k], writes=["mix%d" % m])
                            else:
                                S.pool(lambda e: e.tensor_copy(mix[:, m, :], t1[:]), reads=[t1k], writes=["mix%d" % m])
                        for m in range(8):
                            pa, pak = PS.next()
                            for k in range(8):
                                S.pe(lambda e: e.matmul(pa[:, 0:NB], wout[:, k, m * 128:(m + 1) * 128], mix[:, k, :], start=(k == 0), stop=(k == 7)), reads=["wout", "mix%d" % k], writes=[pak])
                            S.dve(lambda e: e.scalar_tensor_tensor(xT[:, m, :], pa[:, 0:NB], modv[:, 2, m, s:s + 1], xT[:, m, :], ALU.mult, ALU.add), reads=[pak, "modv", "xT%d" % m], writes=["xT%d" % m])
                        r_t, r_k = rb2.next()
                        rms_bc(r_t, r_k)
                        for k in range(8):
                            t1, t1k = tmpC.next()
                            S.dve(lambda e: e.tensor_tensor(t1[:], xT[:, k, :], r_t[:], ALU.mult), reads=["xT%d" % k, r_k], writes=[t1k])
                            S.dve(lambda e: e.tensor_scalar(h2[:, k, :], t1[:], modv[:, 3, k, s:s + 1], modv[:, 4, k, s:s + 1], ALU.mult, ALU.add), reads=[t1k, "modv"], writes=["h2_%d" % k])
                        for i in range(22):
                            pu, puk = PS.next()
                            pz, pzk = PS.next()
                            for k in range(8):
                                S.pe(lambda e: e.matmul(pu[:, 0:NB], wfi[:, k, i * 128:(i + 1) * 128], h2[:, k, :], start=(k == 0), stop=(k == 7)), reads=["wfi", "h2_%d" % k], writes=[puk])
                            for k in range(8):
                                S.pe(lambda e: e.matmul(pz[:, 0:NB], wfi[:, k, DFF + i * 128:DFF + (i + 1) * 128], h2[:, k, :], start=(k == 0), stop=(k == 7)), reads=["wfi", "h2_%d" % k], writes=[pzk])
                            t1, t1k = tmpC.next()
                            S.act(lambda e: e.activation(t1[:], pu[:, 0:NB], AF.Silu), reads=[puk], writes=[t1k])
                            S.dve(lambda e: e.tensor_tensor(aff[:, i, :], pz[:, 0:NB], t1[:], ALU.mult), reads=[pzk, t1k], writes=["aff%d" % i])
                        for m in range(8):
                            f_t, f_k = wfo.next()
                            S.dma("sp", f_t[:], b_fo_v[:, :, m * 128:(m + 1) * 128], writes=[f_k])
                            pa, pak = PS.next()
                            for i in range(22):
                                S.pe(lambda e: e.matmul(pa[:, 0:NB], f_t[:, i, :], aff[:, i, :], start=(i == 0), stop=(i == 21)), reads=[f_k, "aff%d" % i], writes=[pak])
                            S.dve(lambda e: e.scalar_tensor_tensor(xT[:, m, :], pa[:, 0:NB], modv[:, 5, m, s:s + 1], xT[:, m, :], ALU.mult, ALU.add), reads=[pak, "modv", "xT%d" % m], writes=["xT%d" % m])
                        r_t, r_k = rb2.next()
                        rms_bc(r_t, r_k)
                        for k in range(8):
                            S.dve(lambda e: e.scalar_tensor_tensor(xT[:, k, :], xT[:, k, :], vecs[:, V_FN + k:V_FN + k + 1], r_t[:], ALU.mult, ALU.mult), reads=["xT%d" % k, "vecs", r_k], writes=["xT%d" % k])
                        for j in range(NB // 128):
                            y_t, y_k = ytok.next()
                            for kq in range(2):
                                pb, pk = PS.next()
                                for kk in range(4):
                                    k = kq * 4 + kk
                                    S.pe(lambda e: e.transpose(pb[:, kk * 128:(kk + 1) * 128], xT[:, k, j * 128:(j + 1) * 128], identf[:]), reads=["xT%d" % k, "identf"], writes=[pk])
                                cast_any(y_t[:, kq * 512:(kq + 1) * 512], pb[:, :], [pk], [y_k], engines=("act", "dve"))
                            nyo[0] += 1
                            S.dma("pool", ydst(g0 + j * 128, 128), y_t[:], reads=[y_k], writes=["yout%d" % nyo[0]])
                S.barrier()
                S.es = old

        S.finish()
    return nc


def _rope_tables(TM):
    inv = (np.float32(10000.0) ** (-(np.arange(16, dtype=np.float32)) / np.float32(16))).astype(np.float32)
    ang = (np.arange(TM, dtype=np.float32)[None, :] * inv[:, None]).astype(np.float32)
    idx = np.arange(128) % 16
    return np.stack([np.cos(ang)[idx], np.sin(ang)[idx]]).astype(np.float32)


def host_maps(inp, T0, T1, ncores):
    f = lambda a: np.ascontiguousarray(np.asarray(a, dtype=np.float32))
    g = {k: np.asarray(v) for k, v in inp.items()}
    vecs = np.zeros((128, NV), np.float32)

    def put(col, v):
        v = np.asarray(v, np.float32).reshape(-1, 128).T
        vecs[:, col:col + v.shape[1]] = v
    put(V_NMIX, g["norm_mix"][0]); put(V_NFFN, g["norm_ffn"][0]); put(V_FN, g["final_norm"]); put(V_BADA, g["b_ada"][0])
    put(V_QAN, g["q_a_norm"][0]); put(V_KVAN, g["kv_a_norm"][0]); put(V_MU, g["mu_shift"][0])
    put(V_W0, g["w0"][0].reshape(-1)); put(V_A0, g["a0"][0].reshape(-1))
    put(V_KK, g["k_k"][0]); put(V_KA, g["k_a"][0]); put(V_RK, g["r_k"][0].reshape(-1))
    w_in = g["w_in"][0]
    w_uq = g["w_uq"][0].reshape(QL, NH, 96)
    w_ukv = g["w_ukv"][0].reshape(KVL, NH, 128)
    shared = {
        "vecs": vecs,
        "lnwb": f(np.stack([g["lnx_w"][0], g["lnx_b"][0]])),
        "rope": _rope_tables(max(T0, T1)),
        "w_ada": f(g["w_ada"][0]), "w_in": f(w_in),
        "w_kps": f(np.concatenate([w_in[:, 656:672], w_in[:, 640:656]], axis=1)),
        "w_uqn": f(w_uq[:, :, :64].reshape(QL, 512)), "w_uqp": f(w_uq[:, :, 64:].reshape(QL, 256)),
        "w_uqs": f(np.concatenate([w_uq[:, :, 80:96], w_uq[:, :, 64:80]], axis=2).reshape(QL, 256)),
        "w_ukn": f(w_ukv[:, :, :64].reshape(KVL, 512)), "w_ukv": f(w_ukv[:, :, 64:].reshape(KVL, 512)),
        "w_dec": f(g["w_decay_up"][0]), "w_icl": f(g["w_iclr_up"][0]), "w_gat": f(g["w_gate_up"][0]),
        "w_mo": f(g["w_mla_o"][0]), "w_ro": f(g["w_rwkv_o"][0]), "w_out": f(g["w_out"][0]),
        "w_fi": f(g["w_ffn_in"][0]), "w_fo": f(g["w_ffn_out"][0]),
    }
    maps = []
    for c in range(ncores):
        c3 = np.concatenate([g["c_prompt"][c:c + 1], g["c_sample"][2 * c:2 * c + 2]], axis=0)
        cT = np.zeros((128, 8, 4), np.float32)
        cT[:, :, 0:3] = c3.reshape(3, 8, 128).transpose(2, 1, 0)
        m = dict(shared)
        m["x_p"] = f(g["x_prompt"][c])
        m["x_s"] = f(g["x_sample"][2 * c:2 * c + 2].reshape(2 * T1, D))
        m["cT"] = cT
        maps.append(m)
    return maps


_NC_CACHE = {}


def kernel(**inputs):
    T0 = inputs["x_prompt"].shape[1]
    T1 = inputs["x_sample"].shape[1]
    ncores = inputs["x_prompt"].shape[0]
    key = (T0, T1)
    if key not in _NC_CACHE:
        _NC_CACHE[key] = build_nc(T0, T1)
    nc = _NC_CACHE[key]
    maps = host_maps(inputs, T0, T1, ncores)
    res = run_bass_kernel_spmd(nc, maps, core_ids=list(range(ncores)))
    yp = np.stack([np.asarray(r["y_p"], np.float32) for r in res.results])
    ys = np.concatenate([np.asarray(r["y_s"], np.float32).reshape(2, T1, D) for r in res.results], axis=0)
    return (yp, ys)
```

```python
import numpy as np
import os
ASTOP = int(os.environ.get('ASTOP', 99))
RSTOP = int(os.environ.get('RSTOP', 99))
from contextlib import ExitStack
import concourse.bass as bass
import concourse.mybir as mybir
from concourse.bass_utils import run_bass_kernel_spmd

F32 = mybir.dt.float32
BF16 = mybir.dt.bfloat16
AF = mybir.ActivationFunctionType
ALU = mybir.AluOpType
AX = mybir.AxisListType


class Sched:
    SEM_LIMIT = 30000

    def __init__(self, nc, n_dma_sems=20):
        self.nc = nc
        self.es = ExitStack()
        self.es0 = self.es
        self.engs = {"pe": nc.tensor, "act": nc.scalar, "dve": nc.vector, "pool": nc.gpsimd, "sp": nc.sync}
        self.sem = {}
        self.cnt = {}
        self.waited = {e: {} for e in self.engs}
        self.buf = {}
        self.n_dma_sems = n_dma_sems
        self.dma_sems = {}
        self.dma_rr = {}
        self.nsem = 0
        self.ninst = 0
        self.pe_last = {}

    def __enter__(self):
        self.es.__enter__()
        for e in ("pe", "act", "dve", "pool"):
            self._new_sem(e)
        for q in ("sp", "pool", "act"):
            self.dma_sems[q] = [[self._alloc_sem("dq_%s_%d" % (q, i)), 0] for i in range(self.n_dma_sems)]
            self.dma_rr[q] = 0
        return self

    def __exit__(self, *a):
        return self.es.__exit__(*a)

    def _alloc_sem(self, name):
        self.nsem += 1
        return self.es0.enter_context(self.nc.semaphore("%s_%d" % (name, self.nsem)))

    def _new_sem(self, e):
        self.sem[e] = self._alloc_sem("s_" + e)
        self.cnt[e] = 0

    def sb(self, name, shape, dtype):
        return self.es.enter_context(self.nc.sbuf_tensor("sb_" + name, list(shape), dtype))

    def ps(self, name, i=None):
        return self.es.enter_context(self.nc.psum_tensor(name, [128, 512], F32))

    def _b(self, k):
        b = self.buf.get(k)
        if b is None:
            b = self.buf[k] = {"w": None, "r": {}}
        return b

    def _emit(self, E, fn, reads, writes, dma_q=None, rg=None):
        deps = {}

        def add(d):
            if d is None:
                return
            s, v, owner = d
            if owner == "pe" and E == "pe" and dma_q is None:
                return
            key = id(s)
            if key not in deps or deps[key][1] < v:
                deps[key] = (s, v)

        for k in reads:
            b = self._b(k)
            add(b["w"])
            if k.startswith("bank"):
                for d in b["r"].values():
                    if d[2] != E:
                        add(d)
        for k in writes:
            b = self._b(k)
            add(b["w"])
            for d in b["r"].values():
                add(d)
        if E == "pe" and dma_q is None:
            for k in writes:
                if k.startswith("bank"):
                    last = self.pe_last.get(k)
                    if last is not None and last[0] is not None and rg is not None and last[0] != rg:
                        s_, v_, _o = last[1]
                        if id(s_) not in deps or deps[id(s_)][1] < v_:
                            deps[id(s_)] = (s_, v_)
        eng = self.engs[E]
        slot = None
        if dma_q is not None:
            pool = self.dma_sems[E]
            slot = pool[self.dma_rr[E] % len(pool)]
            self.dma_rr[E] += 1
            if slot[1] > 0:
                add((slot[0], 16 * slot[1], "dma"))
        w = self.waited[E]
        for key, (s, v) in deps.items():
            if w.get(key, 0) >= v:
                continue
            eng.wait_ge(s, v)
            w[key] = v
        inst = fn(eng)
        self.ninst += 1
        if dma_q is not None:
            slot[1] += 1
            inst.then_inc(slot[0], 16)
            me = (slot[0], 16 * slot[1], "dma")
        else:
            if self.cnt[E] >= self.SEM_LIMIT:
                self._new_sem(E)
            self.cnt[E] += 1
            inst.then_inc(self.sem[E], 1)
            me = (self.sem[E], self.cnt[E], E)
        for k in reads:
            self._b(k)["r"][id(me[0])] = me
        for k in writes:
            b = self._b(k)
            b["w"] = me
            b["r"] = {}
            if E == "pe" and dma_q is None and k.startswith("bank"):
                self.pe_last[k] = (rg, me)
        return inst

    def pe(self, fn, reads=(), writes=(), rg=None):
        return self._emit("pe", fn, reads, writes, rg=rg)

    def act(self, fn, reads=(), writes=()):
        return self._emit("act", fn, reads, writes)

    def dve(self, fn, reads=(), writes=()):
        return self._emit("dve", fn, reads, writes)

    def pool(self, fn, reads=(), writes=()):
        return self._emit("pool", fn, reads, writes)

    def on(self, E, fn, reads=(), writes=()):
        return self._emit(E, fn, reads, writes)

    def dma(self, q, out, in_, reads=(), writes=(), **kw):
        return self._emit(q, lambda e: e.dma_start(out=out, in_=in_, **kw), reads, writes, dma_q=q)

    def barrier(self):
        self.finish()
        self.buf = {}
        self.pe_last = {}

    def finish(self):
        allw = {}
        for b in self.buf.values():
            ds = list(b["r"].values())
            if b["w"] is not None:
                ds.append(b["w"])
            for (s, v, o) in ds:
                if id(s) not in allw or allw[id(s)][1] < v:
                    allw[id(s)] = (s, v)
        for E in ("sp", "pool", "act", "dve", "pe"):
            for key, (s, v) in allw.items():
                if self.waited[E].get(key, 0) < v:
                    self.engs[E].wait_ge(s, v)
                    self.waited[E][key] = v


D = 1024
NH = 8
QL, KVL, ROPE = 384, 256, 32
INC = 4512
DFF = 2816
EPS = 1e-6
LNX_EPS = 64e-5
C0 = 0.6065306597126334
ATT_SCALE = 96 ** -0.5
V_NMIX, V_NFFN, V_FN, V_BADA, V_QAN, V_KVAN, V_MU, V_W0, V_A0, V_KK, V_KA, V_OMKA_, V_RK = 0, 8, 16, 24, 72, 75, 77, 91, 99, 107, 111, 115, 115
NV = 119


class Ring:
    def __init__(self, S, name, n, shape, dtype):
        self.t = [S.sb("%s%d" % (name, i), shape, dtype) for i in range(n)]
        self.k = ["%s%d" % (name, i) for i in range(n)]
        self.i = 0

    def next(self):
        j = self.i % len(self.t)
        self.i += 1
        return self.t[j], self.k[j]


class PRing:
    def __init__(self, banks, keys):
        self.t, self.k, self.i = banks, keys, 0

    def next(self):
        j = self.i % len(self.t)
        self.i += 1
        return self.t[j], self.k[j]


def build_nc(T0, T1, phases="0ABRC", dbg=False):
    seqT = [T0, T1, T1]
    NT = T0 + 2 * T1
    soff = [0, T0, T0 + T1]
    TM = max(T0, T1)
    nc = bass.Bass("TRN2", target_bir_lowering=False)
    dr = lambda n, s, dt=F32, kind="ExternalInput": nc.dram_tensor(n, list(s), dt, kind=kind).ap()
    x_p = dr("x_p", [T0, D])
    x_s = dr("x_s", [2 * T1, D])
    xsrc = lambda g0, n: (x_p[g0:g0 + n] if g0 < T0 else x_s[g0 - T0:g0 - T0 + n])
    y_p = dr("y_p", [T0, D], kind="ExternalOutput")
    y_s = dr("y_s", [2 * T1, D], kind="ExternalOutput")
    ydst = lambda g0, n: (y_p[g0:g0 + n] if g0 < T0 else y_s[g0 - T0:g0 - T0 + n])
    cT_d = dr("cT", [128, 8, 4])
    vecs_d = dr("vecs", [128, NV])
    lnwb_d = dr("lnwb", [2, 512])
    rope_d = dr("rope", [2, 128, TM])
    w_ada = dr("w_ada", [D, 6 * D])
    w_in = dr("w_in", [D, INC])
    w_kps = dr("w_kps", [D, 32])
    w_uqn = dr("w_uqn", [QL, 512]); w_uqp = dr("w_uqp", [QL, 256]); w_uqs = dr("w_uqs", [QL, 256])
    w_ukn = dr("w_ukn", [KVL, 512]); w_ukv = dr("w_ukv", [KVL, 512])
    w_dec = dr("w_dec", [2, 64, 512]); w_icl = dr("w_icl", [2, 64, 512]); w_gat = dr("w_gat", [128, 512])
    w_mo = dr("w_mo", [512, D]); w_ro = dr("w_ro", [512, D]); w_out = dr("w_out", [D, D])
    w_fi = dr("w_fi", [D, 2 * DFF]); w_fo = dr("w_fo", [DFF, D])
    it = lambda n, s, dt=BF16: dr(n, s, dt, kind=("ExternalOutput" if dbg else "Internal"))
    b_in = it("b_in", [D, INC + 32])
    b_uqn = it("b_uqn", [QL, 512]); b_uqp = it("b_uqp", [QL, 256]); b_uqr = it("b_uqr", [QL, 256])
    b_ukn = it("b_ukn", [KVL, 512]); b_ukv = it("b_ukv", [KVL, 512])
    b_dec = it("b_dec", [128, 512]); b_icl = it("b_icl", [128, 512]); b_gat = it("b_gat", [128, 512])
    b_mo = it("b_mo", [512, D]); b_ro = it("b_ro", [512, D]); b_out = it("b_out", [D, D])
    b_fi = it("b_fi", [D, 2 * DFF]); b_fo = it("b_fo", [8, 128, 22, 128])
    qT_d = it("qT_d", [NH, 96, NT]); kT_d = it("kT_d", [NH, 64, NT]); kpe_d = it("kpe_d", [32, NT])
    V_d = it("V_d", [NT // 128, 128, NH * 65])
    NTP = NT + 6
    pR_d = it("pR_d", [1792, NTP])
    gT_d = it("gT_d", [2048, NT])
    atT_d = it("atT_d", [512, NT]); rwT_d = it("rwT_d", [512, NT])
    y0_d = it("y0_d", [NT, 512], F32)
    ppos = lambda s: soff[s] + 2 * s

    S = Sched(nc)
    with S:
        identb = S.sb("identb", [128, 128], BF16)
        identf = S.sb("identf", [128, 128], F32)
        onesb = S.sb("onesb", [128, 128], BF16)
        vecs = S.sb("vecs", [128, NV], F32)
        modv = S.sb("modv", [128, 6, 8, 4], F32)
        banks = [S.ps("bank%d" % i) for i in range(8)]
        bkeys = ["bank%d" % i for i in range(8)]
        PS = PRing(banks, bkeys)
        CONST = ["identb", "identf", "onesb", "vecs", "modv"]

        def ident_build(t, key):
            S.pool(lambda e: e.memset(t[:], 1.0), writes=[key])
            S.pool(lambda e: e.affine_select(t[:], t[:], [[-1, 128]], ALU.is_equal, 0.0, base=0, channel_multiplier=1), reads=[key], writes=[key])
        ident_build(identb, "identb")
        ident_build(identf, "identf")
        S.pool(lambda e: e.memset(onesb[:], 1.0), writes=["onesb"])
        S.dma("sp", vecs[:], vecs_d, writes=["vecs"])

        rr = [0]

        def cast_any(out, in_, reads, writes, scale=None, engines=("act", "dve", "pool")):
            E = engines[rr[0] % len(engines)]
            rr[0] += 1
            if E == "act":
                if scale is None:
                    S.act(lambda e: e.activation(out, in_, AF.Copy), reads=reads, writes=writes)
                else:
                    S.act(lambda e: e.mul(out, in_, scale), reads=reads, writes=writes)
            else:
                if scale is None:
                    S.on(E, lambda e: e.tensor_copy(out, in_), reads=reads, writes=writes)
                else:
                    S.on(E, lambda e: e.tensor_scalar(out, in_, scale, None, ALU.mult), reads=reads, writes=writes)

        if "0" in phases:
            with ExitStack() as sc:
                S.es, old = sc, S.es
                stg = Ring(S, "stg", 3, [128, 2048], F32)
                stb = Ring(S, "stb", 3, [128, 2048], BF16)
                nwr = [0]

                def cast_w(src, dst, R, Cc, neg_cols=None):
                    for r0 in range(0, R, 128):
                        rows = min(128, R - r0)
                        for c0 in range(0, Cc, 2048):
                            cw = min(2048, Cc - c0)
                            a, ak = stg.next()
                            b, bk = stb.next()
                            S.dma("sp", a[0:rows, 0:cw], src[r0:r0 + rows, c0:c0 + cw], writes=[ak])
                            if neg_cols is None:
                                cast_any(b[0:rows, 0:cw], a[0:rows, 0:cw], [ak], [bk])
                            else:
                                av = a[0:rows, 0:cw].rearrange("p (g t) -> p g t", t=32)
                                bv = b[0:rows, 0:cw].rearrange("p (g t) -> p g t", t=32)
                                S.act(lambda e: e.mul(bv[:, :, 0:16], av[:, :, 0:16], -1.0), reads=[ak], writes=[bk])
                                S.dve(lambda e: e.tensor_copy(bv[:, :, 16:32], av[:, :, 16:32]), reads=[ak], writes=[bk])
                            nwr[0] += 1
                            S.dma("pool", dst[r0:r0 + rows, c0:c0 + cw], b[0:rows, 0:cw], reads=[bk], writes=["wscr%d" % nwr[0]])
                cast_w(w_in, b_in[:, 0:INC], D, INC)
                cast_w(w_kps, b_in[:, INC:INC + 32], D, 32, neg_cols=True)
                cast_w(w_uqn, b_uqn, QL, 512); cast_w(w_uqp, b_uqp, QL, 256); cast_w(w_uqs, b_uqr, QL, 256, neg_cols=True)
                cast_w(w_ukn, b_ukn, KVL, 512); cast_w(w_ukv, b_ukv, KVL, 512)
                cast_w(w_dec.rearrange("d k n -> (d k) n"), b_dec, 128, 512)
                cast_w(w_icl.rearrange("d k n -> (d k) n"), b_icl, 128, 512)
                cast_w(w_gat, b_gat, 128, 512)
                cast_w(w_mo, b_mo, 512, D); cast_w(w_ro, b_ro, 512, D); cast_w(w_out, b_out, D, D)
                cast_w(w_fi, b_fi, D, 2 * DFF)
                for k in range(22):
                    a, ak = stg.next()
                    b, bk = stb.next()
                    S.dma("sp", a[:, 0:D], w_fo[k * 128:(k + 1) * 128, :], writes=[ak])
                    cast_any(b[:, 0:D], a[:, 0:D], [ak], [bk])
                    nwr[0] += 1
                    S.dma("pool", b_fo[:, :, k, :].rearrange("m p n -> p m n"), b[:, 0:D].rearrange("p (m n) -> p m n", n=128), reads=[bk], writes=["wscr%d" % nwr[0]])
                zt = S.sb("zt", [128, 2], BF16)
                S.pool(lambda e: e.memset(zt[:], 0.0), writes=["zt"])
                for s in range(3):
                    for c in (ppos(s), ppos(s) + seqT[s] + 1):
                        for r0 in range(0, 1792, 128):
                            nwr[0] += 1
                            S.dma("pool", pR_d[r0:r0 + 128, c:c + 1], zt[:, 0:1], reads=["zt"], writes=["wscr%d" % nwr[0]], allow_slow_non_contiguous=True)
                cT = S.sb("cT", [128, 8, 4], F32)
                sT = S.sb("sT", [128, 8, 4], F32)
                modT = S.sb("modT", [128, 48, 4], F32)
                S.dma("sp", cT[:], cT_d, writes=["cT"])
                S.act(lambda e: e.activation(sT[:], cT[:], AF.Silu), reads=["cT"], writes=["sT"])
                wa = Ring(S, "wa", 2, [128, 8, 512], F32)
                w_ada_v = w_ada.rearrange("(k p) n -> p k n", p=128)
                pb, pk = PS.next()
                for piece in range(12):
                    a, ak = wa.next()
                    S.dma("sp", a[:], w_ada_v[:, :, piece * 512:(piece + 1) * 512], writes=[ak])
                    for mm_ in range(4):
                        m = piece * 4 + mm_
                        for k in range(8):
                            S.pe(lambda e: e.matmul(pb[:, m * 4:m * 4 + 4], a[:, k, mm_ * 128:(mm_ + 1) * 128], sT[:, k, :], start=(k == 0), stop=(k == 7)),
                                 reads=[ak, "sT"], writes=[pk])
                S.dve(lambda e: e.tensor_tensor(modT[:], pb[:, 0:192].rearrange("p (m s) -> p m s", s=4),
                                                vecs[:, V_BADA:V_BADA + 48].unsqueeze(2).to_broadcast([128, 48, 4]), ALU.add),
                      reads=[pk, "vecs"], writes=["modT"])
                for (dst, src, nv) in ((0, 8, V_NMIX), (3, 32, V_NFFN)):
                    S.dve(lambda e: e.tensor_scalar(modv[:, dst], modT[:, src:src + 8, :], 1.0, None, ALU.add), reads=["modT"], writes=["modv"])
                    S.dve(lambda e: e.tensor_tensor(modv[:, dst], modv[:, dst], vecs[:, nv:nv + 8].unsqueeze(2).to_broadcast([128, 8, 4]), ALU.mult),
                          reads=["modv", "vecs"], writes=["modv"])
                for (dst, src) in ((1, 0), (2, 16), (4, 24), (5, 40)):
                    S.dve(lambda e: e.tensor_copy(modv[:, dst], modT[:, src:src + 8, :]), reads=["modT"], writes=["modv"])
                S.barrier()
                S.es = old

        if "A" in phases:
            with ExitStack() as sc:
                S.es, old = sc, S.es
                win = S.sb("win", [128, 8, INC + 32], BF16)
                wuqn = S.sb("wuqn", [128, 3, 512], BF16); wuqp = S.sb("wuqp", [128, 3, 256], BF16); wuqr = S.sb("wuqr", [128, 3, 256], BF16)
                wukn = S.sb("wukn", [128, 2, 512], BF16); wukv = S.sb("wukv", [128, 2, 512], BF16)
                WA = ["win", "wuqn", "wuqp", "wuqr", "wukn", "wukv"]
                b_in_v = b_in.rearrange("(k p) n -> p k n", p=128)
                for k in range(8):
                    S.dma("sp", win[:, k, :], b_in_v[:, k, :], writes=["win"])
                for (t, src, key) in ((wuqn, b_uqn, "wuqn"), (wuqp, b_uqp, "wuqp"), (wuqr, b_uqr, "wuqr"), (wukn, b_ukn, "wukn"), (wukv, b_ukv, "wukv")):
                    S.dma("sp", t[:], src.rearrange("(k p) n -> p k n", p=128), writes=[key])
                xt = Ring(S, "xt", 3, [128, D], F32)
                junk = S.sb("junk", [128, D], BF16)
                ssq = Ring(S, "ssq", 2, [128, 8], F32)
                xsbr = Ring(S, "xsb", 2, [128, 4, D], BF16)
                hT = S.sb("hT", [128, 8, 512], BF16)
                cqg = S.sb("cqg", [128, 5, 512], BF16)
                sq = S.sb("sq", [128, 5, 512], BF16)
                rbc = S.sb("rbc", [128, 2, 512], F32)
                rtok = S.sb("rtok", [128, 8], F32)
                cs = Ring(S, "cs", 2, [128, 2, 512], F32)
                csr = S.sb("csr", [128, 2, 512], F32)
                tmpA = Ring(S, "tmpA", 4, [128, 512], F32)
                prw = Ring(S, "prw", 1, [128, 14, 512], BF16)
                gts = Ring(S, "gts", 1, [128, 16, 512], BF16)
                qo = Ring(S, "qo", 1, [128, 6, 512], BF16)
                ko = Ring(S, "ko", 1, [128, 5, 512], BF16)
                vo = Ring(S, "vo", 2, [128, 4, NH, 65], BF16)
                for i in range(len(vo.t)):
                    S.pool(lambda e: e.memset(vo.t[i][:], 1.0), writes=[vo.k[i]])
                nst = [0]
                print("phase A sbuf bytes remaining", nc.sbuf_bytes_remaining)

                def store(dst, src, reads):
                    nst[0] += 1
                    S.dma("sp", dst, src, reads=reads, writes=["ascr%d" % nst[0]])

                chunks = ([("cq", i * 128, 128, i) for i in range(3)] + [("ckv", 384 + i * 128, 128, i) for i in range(2)]
                          + [("kpe", 640, 32, 0), ("kpr", INC, 32, 0)]
                          + [("rw", 672 + i * 128, 128, i) for i in range(14)] + [("gate", 2464 + i * 128, 128, i) for i in range(16)])
                askip = os.environ.get('ASKIP', '').split(',')
                chunks = [c for c in chunks if c[0] not in askip]
                def stage1(s, blk):
                    l0 = blk * 512
                    g0 = soff[s] + l0
                    c_t, c_k = cs.next()
                    S.dma("sp", c_t[:], rope_d[:, :, l0:l0 + 512].rearrange("c p t -> p c t"), writes=[c_k])
                    sq_t, sq_k = ssq.next()
                    xsb, xsbk = xsbr.next()
                    for j in range(4):
                        x_t, x_k = xt.next()
                        S.dma("sp", x_t[:], xsrc(g0 + j * 128, 128), writes=[x_k])
                        S.act(lambda e: e.activation(junk[:], x_t[:], AF.Square, accum_out=sq_t[:, j:j + 1]), reads=[x_k], writes=["junk", sq_k])
                        S.act(lambda e: e.activation(sq_t[:, 4 + j:5 + j], sq_t[:, j:j + 1], AF.Sqrt, bias=EPS, scale=1.0 / D), reads=[sq_k], writes=[sq_k])
                        S.dve(lambda e: e.reciprocal(sq_t[:, 4 + j:5 + j], sq_t[:, 4 + j:5 + j]), reads=[sq_k], writes=[sq_k])
                        S.dve(lambda e: e.tensor_scalar(xsb[:, j, :], x_t[:], sq_t[:, 4 + j:5 + j], None, ALU.mult), reads=[x_k, sq_k], writes=[xsbk + "_%d" % j])
                    return (c_t, c_k, xsb, xsbk)

                blist = [(s, blk) for s in range(3) for blk in range(seqT[s] // 512)]
                nxt = stage1(*blist[0])
                for bidx, (s, blk) in enumerate(blist):
                    if True:
                        l0 = blk * 512
                        g0 = soff[s] + l0
                        c_t, c_k, xsb, xsbk = nxt
                        for k in range(8):
                            pb, pk = PS.next()
                            pbb = pb[:].bitcast(BF16)
                            for j in range(4):
                                S.pe(lambda e: e.transpose(pbb[:, j * 128:(j + 1) * 128], xsb[:, j, k * 128:(k + 1) * 128], identb[:]), reads=[xsbk + "_%d" % j, "identb"], writes=[pk])
                            S.dve(lambda e: e.tensor_scalar(hT[:, k, :], pbb[:, 0:512], modv[:, 0, k, s:s + 1], modv[:, 1, k, s:s + 1], ALU.mult, ALU.add),
                                  reads=[pk, "modv"], writes=["hT%d" % k])
                        if bidx + 1 < len(blist):
                            nxt = stage1(*blist[bidx + 1])
                        p_t, p_k = prw.next()
                        g_t, g_k = gts.next()
                        q_t, q_k = qo.next()
                        k_t, k_k = ko.next()
                        v_t, v_k = vo.next()
                        kpe_ps = {}
                        for (kind, c0, M, i) in chunks:
                            pb, pk = PS.next()
                            for k in range(8):
                                S.pe(lambda e: e.matmul(pb[0:M, :], win[:, k, c0:c0 + M], hT[:, k, :], start=(k == 0), stop=(k == 7)), reads=["win", "hT%d" % k], writes=[pk])
                            AOPS = int(os.environ.get('AOPS', 7))
                            if kind in ("cq", "ckv") and AOPS != 7:
                                ii = i if kind == "cq" else 3 + i
                                vcol = (V_QAN + i) if kind == "cq" else (V_KVAN + i)
                                if AOPS & 2:
                                    S.act(lambda e: e.activation(sq[:, ii, :], pb[:, :], AF.Square), reads=[pk], writes=["sq%d" % ii])
                                if AOPS & 4:
                                    S.dve(lambda e: e.tensor_scalar(cqg[:, ii, :], pb[:, :], vecs[:, vcol:vcol + 1], None, ALU.mult), reads=[pk, "vecs"], writes=["cqg%d" % ii])
                            elif kind in ("cq", "ckv"):
                                ii = i if kind == "cq" else 3 + i
                                vcol = (V_QAN + i) if kind == "cq" else (V_KVAN + i)
                                S.act(lambda e: e.activation(sq[:, ii, :], pb[:, :], AF.Square), reads=[pk], writes=["sq%d" % ii])
                                S.dve(lambda e: e.tensor_scalar(cqg[:, ii, :], pb[:, :], vecs[:, vcol:vcol + 1], None, ALU.mult), reads=[pk, "vecs"], writes=["cqg%d" % ii])
                            elif kind == "kpe":
                                kpe_ps["a"] = (pb, pk)
                            elif kind == "kpr":
                                pa, pak = kpe_ps["a"]
                                t1, t1k = tmpA.next()
                                t2, t2k = tmpA.next()
                                S.dve(lambda e: e.tensor_tensor(t1[0:32, :], pa[0:32, :], c_t[0:32, 0, :], ALU.mult), reads=[pak, c_k], writes=[t1k])
                                S.dve(lambda e: e.tensor_tensor(t2[0:32, :], pb[0:32, :], c_t[0:32, 1, :], ALU.mult), reads=[pk, c_k], writes=[t2k])
                                S.pool(lambda e: e.tensor_tensor(k_t[0:32, 4, :], t1[0:32, :], t2[0:32, :], ALU.add), reads=[t1k, t2k], writes=[k_k])
                                store(kpe_d[:, g0:g0 + 512], k_t[0:32, 4, :], [k_k])
                            elif kind == "rw":
                                cast_any(p_t[:, i, :], pb[:, :], [pk], [p_k], engines=("act", "dve"))
                            else:
                                S.act(lambda e: e.activation(g_t[:, i, :], pb[:, :], AF.Sigmoid), reads=[pk], writes=[g_k])
                        if ASTOP <= 2:
                            continue
                        cp = ppos(s) + 1 + l0
                        store(pR_d[:, cp:cp + 512].rearrange("(i p) t -> p i t", p=128), p_t[:], [p_k])
                        store(gT_d[:, g0:g0 + 512].rearrange("(i p) t -> p i t", p=128), g_t[:], [g_k])
                        if ASTOP <= 3:
                            continue
                        for (which, i0, n, dim) in ((0, 0, 3, QL), (1, 3, 2, KVL)):
                            pb, pk = PS.next()
                            for i in range(n):
                                S.pe(lambda e: e.matmul(pb[:, :], onesb[:], sq[:, i0 + i, :], start=(i == 0), stop=(i == n - 1)), reads=["onesb", "sq%d" % (i0 + i)], writes=[pk])
                            S.act(lambda e: e.activation(rbc[:, which, :], pb[:, :], AF.Sqrt, bias=EPS, scale=1.0 / dim), reads=[pk], writes=["rbc"])
                            S.dve(lambda e: e.reciprocal(rbc[:, which, :], rbc[:, which, :]), reads=["rbc"], writes=["rbc"])
                        pb, pk = PS.next()
                        for j in range(4):
                            for i in range(2):
                                S.pe(lambda e: e.matmul(pb[:, 2 * j:2 * j + 2], sq[:, 3 + i, j * 128:(j + 1) * 128], onesb[:, 0:2], start=(i == 0), stop=(i == 1)),
                                     reads=["sq%d" % (3 + i), "onesb"], writes=[pk])
                        S.act(lambda e: e.activation(rtok[:], pb[:, 0:8], AF.Sqrt, bias=EPS, scale=1.0 / KVL), reads=[pk], writes=["rtok"])
                        S.dve(lambda e: e.reciprocal(rtok[:], rtok[:]), reads=["rtok"], writes=["rtok"])
                        S.pool(lambda e: e.tensor_tensor(csr[:], c_t[:], rbc[:, 0:1, :].to_broadcast([128, 2, 512]), ALU.mult), reads=[c_k, "rbc"], writes=["csr"])
                        if ASTOP <= 4:
                            continue
                        for c in range(4):
                            pb, pk = PS.next()
                            for kc in range(3):
                                S.pe(lambda e: e.matmul(pb[:, :], wuqn[:, kc, c * 128:(c + 1) * 128], cqg[:, kc, :], start=(kc == 0), stop=(kc == 2)), reads=["wuqn", "cqg%d" % kc], writes=[pk])
                            S.dve(lambda e: e.tensor_tensor(q_t[:, c, :], pb[:, :], rbc[:, 0, :], ALU.mult), reads=[pk, "rbc"], writes=[q_k])
                            for i in range(2):
                                store(qT_d[2 * c + i, 0:64, g0:g0 + 512], q_t[64 * i:64 * i + 64, c, :], [q_k])
                        for c in range(2):
                            pa, pak = PS.next()
                            pb, pk = PS.next()
                            for kc in range(3):
                                S.pe(lambda e: e.matmul(pa[:, :], wuqp[:, kc, c * 128:(c + 1) * 128], cqg[:, kc, :], start=(kc == 0), stop=(kc == 2)), reads=["wuqp", "cqg%d" % kc], writes=[pak])
                            for kc in range(3):
                                S.pe(lambda e: e.matmul(pb[:, :], wuqr[:, kc, c * 128:(c + 1) * 128], cqg[:, kc, :], start=(kc == 0), stop=(kc == 2)), reads=["wuqr", "cqg%d" % kc], writes=[pk])
                            t1, t1k = tmpA.next()
                            t2, t2k = tmpA.next()
                            S.dve(lambda e: e.tensor_tensor(t1[:], pa[:, :], csr[:, 0, :], ALU.mult), reads=[pak, "csr"], writes=[t1k])
                            S.dve(lambda e: e.tensor_tensor(t2[:], pb[:, :], csr[:, 1, :], ALU.mult), reads=[pk, "csr"], writes=[t2k])
                            S.pool(lambda e: e.tensor_tensor(q_t[:, 4 + c, :], t1[:], t2[:], ALU.add), reads=[t1k, t2k], writes=[q_k])
                            for i in range(4):
                                store(qT_d[4 * c + i, 64:96, g0:g0 + 512], q_t[32 * i:32 * i + 32, 4 + c, :], [q_k])
                        if ASTOP <= 5:
                            continue
                        for c in range(4):
                            pb, pk = PS.next()
                            for kc in range(2):
                                S.pe(lambda e: e.matmul(pb[:, :], wukn[:, kc, c * 128:(c + 1) * 128], cqg[:, 3 + kc, :], start=(kc == 0), stop=(kc == 1)), reads=["wukn", "cqg%d" % (3 + kc)], writes=[pk])
                            S.dve(lambda e: e.tensor_tensor(k_t[:, c, :], pb[:, :], rbc[:, 1, :], ALU.mult), reads=[pk, "rbc"], writes=[k_k])
                            for i in range(2):
                                store(kT_d[2 * c + i, :, g0:g0 + 512], k_t[64 * i:64 * i + 64, c, :], [k_k])
                        for j in range(4):
                            pb, pk = PS.next()
                            for kc in range(2):
                                S.pe(lambda e: e.matmul(pb[:, :], cqg[:, 3 + kc, j * 128:(j + 1) * 128], wukv[:, kc, :], start=(kc == 0), stop=(kc == 1)), reads=["wukv", "cqg%d" % (3 + kc)], writes=[pk])
                            S.dve(lambda e: e.tensor_scalar(v_t[:, j, :, 0:64], pb[:, :].rearrange("p (h d) -> p h d", d=64), rtok[:, 2 * j:2 * j + 1], None, ALU.mult),
                                  reads=[pk, "rtok"], writes=[v_k])
                        store(V_d[g0 // 128:g0 // 128 + 4].rearrange("n p c -> p n c"), v_t[:].rearrange("p n h d -> p n (h d)"), [v_k])
                S.barrier()
                S.es = old

        if "B" in phases:
            with ExitStack() as sc:
                S.es, old = sc, S.es
                NKT = TM // 128
                Vsb = S.sb("Vsb", [128, NKT, NH * 65], BF16)
                kTr = Ring(S, "kTh", 2, [96, TM], BF16)
                qr = Ring(S, "qh", 3, [96, 512], BF16)
                pTr = Ring(S, "pT", 4, [128, 512], BF16)
                osb = Ring(S, "osb", 2, [65, 512], F32)
                rden = Ring(S, "rden", 2, [64, 512], F32)
                ao = Ring(S, "ao", 2, [64, 512], BF16)
                sel65 = S.sb("sel65", [65, 64], F32)
                S.pool(lambda e: e.memset(sel65[:], 0.0), writes=["sel65"])
                S.pool(lambda e: e.memset(sel65[64:65, :], 1.0), writes=["sel65"])
                PSS = PRing(banks[0:4], bkeys[0:4])
                PO = PRing(banks[4:6], bkeys[4:6])
                PD = PRing(banks[6:8], bkeys[6:8])
                nstB = [0]
                for s in range(3):
                    T = seqT[s]
                    nkt = T // 128
                    for t0 in range(0, nkt, 8):
                        nn = min(8, nkt - t0)
                        S.dma("sp", Vsb[:, t0:t0 + nn, :], V_d[soff[s] // 128 + t0:soff[s] // 128 + t0 + nn].rearrange("n p c -> p n c"), writes=["Vsb%d" % (t0 // 8)])
                    for h in range(NH):
                        kt_t, kt_k = kTr.next()
                        S.dma("sp", kt_t[0:64, 0:T], kT_d[h, :, soff[s]:soff[s] + T], writes=[kt_k])
                        S.dma("sp", kt_t[64:96, 0:T], kpe_d[:, soff[s]:soff[s] + T], writes=[kt_k])
                        for qb in range(T // 512):
                            g0 = soff[s] + qb * 512
                            q_t, q_k = qr.next()
                            S.dma("sp", q_t[:], qT_d[h, :, g0:g0 + 512], writes=[q_k])
                            po, pok = PO.next()
                            LA = 2
                            pend = []
                            for i in range(nkt + LA):
                                if i < nkt:
                                    ps_, psk = PSS.next()
                                    S.pe(lambda e: e.matmul(ps_[:, :], kt_t[:, i * 128:(i + 1) * 128], q_t[:, :], start=True, stop=True), reads=[kt_k, q_k], writes=[psk])
                                    p_t, p_k = pTr.next()
                                    S.act(lambda e: e.activation(p_t[:], ps_[:, :], AF.Exp, scale=ATT_SCALE), reads=[psk], writes=[p_k])
                                    pend.append((i, p_t, p_k))
                                if i >= LA:
                                    (kt, pp_t, pp_k) = pend.pop(0)
                                    S.pe(lambda e: e.matmul(po[0:65, :], Vsb[:, kt, h * 65:(h + 1) * 65], pp_t[:], start=(kt == 0), stop=(kt == nkt - 1)),
                                         reads=["Vsb%d" % (kt // 8), pp_k], writes=[pok])
                            o_t, o_k = osb.next()
                            S.dve(lambda e: e.tensor_copy(o_t[:], po[0:65, :]), reads=[pok], writes=[o_k])
                            pd, pdk = PD.next()
                            S.pe(lambda e: e.matmul(pd[0:64, :], sel65[:, :], o_t[:], start=True, stop=True), reads=["sel65", o_k], writes=[pdk])
                            r_t, r_k = rden.next()
                            S.dve(lambda e: e.reciprocal(r_t[:], pd[0:64, :]), reads=[pdk], writes=[r_k])
                            a_t, a_k = ao.next()
                            S.dve(lambda e: e.tensor_tensor(a_t[:], o_t[0:64, :], r_t[:], ALU.mult), reads=[o_k, r_k], writes=[a_k])
                            nstB[0] += 1
                            S.dma("pool", atT_d[h * 64:(h + 1) * 64, g0:g0 + 512], a_t[:], reads=[a_k], writes=["bscr%d" % nstB[0]])
                S.barrier()
                S.es = old


        if "R" in phases:
            with ExitStack() as sc:
                S.es, old = sc, S.es
                PS4 = PRing(banks[0:5], bkeys[0:5])
                PH = PRing(banks[5:6], bkeys[5:6])
                PY = PRing(banks[6:8], bkeys[6:8])
                def tri(name, pattern, cm, op):
                    t = S.sb(name, [128, 128], BF16)
                    S.pool(lambda e: e.memset(t[:], 1.0), writes=[name])
                    S.pool(lambda e: e.affine_select(t[:], t[:], pattern, op, 0.0, base=0, channel_multiplier=cm), reads=[name], writes=[name])
                    S.pool(lambda e: e.memset(t[0:64, 64:128], 0.0), reads=[name], writes=[name])
                    S.pool(lambda e: e.memset(t[64:128, 0:64], 0.0), reads=[name], writes=[name])
                    return t
                mSU = tri("mSU", [[1, 128]], -1, ALU.is_gt)
                mUI = tri("mUI", [[1, 128]], -1, ALU.is_ge)
                mSL = tri("mSL", [[-1, 128]], 1, ALU.is_gt)
                mLI = tri("mLI", [[-1, 128]], 1, ALU.is_ge)
                M12 = []
                for d_, (ma, mb, na, nb) in enumerate(((mSU, mUI, "mSU", "mUI"), (mSL, mLI, "mSL", "mLI"))):
                    t = S.sb("M12_%d" % d_, [128, 4, 128], BF16)
                    for i_ in range(4):
                        src, sk = ((ma, na), (mb, nb))[i_ % 2]
                        S.pool(lambda e: e.tensor_copy(t[:, i_, :], src[:]), reads=[sk], writes=["M12"])
                    M12.append(t)
                M3 = [mSL, mSU]
                M3k = ["mSL", "mSU"]
                CM = S.sb("CM", [64, 2, 128], BF16)
                S.pool(lambda e: e.memset(CM[:], 0.0), writes=["CM"])
                S.pool(lambda e: e.memset(CM[:, 0, 0:64], 1.0), reads=["CM"], writes=["CM"])
                S.pool(lambda e: e.memset(CM[:, 1, 64:128], 1.0), reads=["CM"], writes=["CM"])
                HIND = S.sb("HIND", [128, 4, 8], BF16)
                S.pool(lambda e: e.memset(HIND[:], 0.0), writes=["HIND"])
                for pr in range(4):
                    for hf in range(2):
                        S.pool(lambda e: e.memset(HIND[64 * hf:64 * hf + 64, pr, 2 * pr + hf:2 * pr + hf + 1], 1.0), reads=["HIND"], writes=["HIND"])
                BD1 = S.sb("BD1", [128, 128], BF16)
                S.pool(lambda e: e.memset(BD1[:], 0.0), writes=["BD1"])
                S.pool(lambda e: e.memset(BD1[0:64, 0:64], 1.0), reads=["BD1"], writes=["BD1"])
                S.pool(lambda e: e.memset(BD1[64:128, 64:128], 1.0), reads=["BD1"], writes=["BD1"])
                msk01 = S.sb("msk01", [128, 8, 64], F32)
                S.pool(lambda e: e.memset(msk01[:], 1.0), writes=["msk01"])
                S.pool(lambda e: e.memset(msk01[:, :, 0:1], 0.0), reads=["msk01"], writes=["msk01"])
                lnwb = S.sb("lnwb", [128, 2, 512], F32)
                for i_ in range(2):
                    S.dma("sp", lnwb[:, i_, :], lnwb_d[i_:i_ + 1, :].to_broadcast([128, 512]), writes=["lnwb"])
                wdec = S.sb("wdec", [64, 2, 512], BF16)
                wicl = S.sb("wicl", [128, 2, 512], BF16)
                wgat = S.sb("wgat", [128, 512], BF16)
                S.dma("sp", wdec[:], b_dec.rearrange("(d k) n -> k d n", d=2), writes=["wdec"])
                S.dma("sp", wicl[64:128], b_icl.rearrange("(d k) n -> k d n", d=2), writes=["wicl"])
                S.dma("sp", wgat[:], b_gat, writes=["wgat"])
                omka = S.sb("omka", [128, 4], F32)
                muh = S.sb("muh", [128, 14], F32); omm = S.sb("omm", [128, 14], F32)
                S.dve(lambda e: e.tensor_scalar(muh[:], vecs[:, V_MU:V_MU + 14], 0.5, None, ALU.mult), reads=["vecs"], writes=["muh"])
                S.dve(lambda e: e.tensor_scalar(omm[:], vecs[:, V_MU:V_MU + 14], -1.0, 1.0, ALU.mult, ALU.add), reads=["vecs"], writes=["omm"])
                S.dve(lambda e: e.tensor_scalar(omka[:], vecs[:, V_KA:V_KA + 4], -1.0, 1.0, ALU.mult, ALU.add), reads=["vecs"], writes=["omka"])
                RC = ["mSU", "mUI", "mSL", "mLI", "M12", "CM", "HIND", "BD1", "msk01", "lnwb", "wdec", "wicl", "wgat", "omka"]
                pin = Ring(S, "pin", 1, [128, 4, 514], BF16)
                xs = S.sb("xs", [128, 8, 512], F32)
                tn = {}
                for nm in ("sg", "E", "E2", "tmp", "Gx", "G", "Gi", "Dd", "aa", "ao_", "kk", "rs", "kkn", "tt", "kd", "kdo", "ka"):
                    tn[nm] = S.sb("r_" + nm, [128, 512], BF16 if nm in ("Gx", "G", "Gi", "Dd", "aa", "ao_", "kkn", "tt", "kd", "kdo", "ka") else F32)
                sqk = S.sb("sqk", [128, 512], BF16)
                twsa = S.sb("twsa", [128, 512], BF16)
                ARr = Ring(S, "AR", 2, [128, 4, 4, 2, 128], BF16)
                BTr = Ring(S, "BT", 2, [128, 4, 512], BF16); KTr = Ring(S, "KT", 2, [128, 4, 512], BF16)
                BHr = Ring(S, "BH", 2, [128, 4, 512], BF16); KHr = Ring(S, "KH", 2, [128, 4, 512], BF16)
                VTr = Ring(S, "VT", 2, [128, 4, 512], BF16)
                RKr = Ring(S, "RK", 2, [128, 4, 512], BF16)
                sgdr = Ring(S, "sgd", 2, [128, 512], BF16)
                GCr = Ring(S, "GC", 2, [64, 8, 8], F32)
                GEnd = S.sb("GEnd", [128, 4, 8], F32)
                ghi = S.sb("ghi", [128, 4, 8], BF16); glo = S.sb("glo", [128, 4, 8], BF16); gdf = S.sb("gdf", [128, 4, 8], F32)
                sb12 = Ring(S, "sb12", 2, [128, 8, 512], BF16)
                Pr = Ring(S, "Pl", 6, [128, 4, 128], BF16)
                Rr = Ring(S, "Rl", 6, [128, 4, 128], BF16)
                TOKr = Ring(S, "TOK", 2, [128, 8, 256], BF16)
                Zr = Ring(S, "Zc", 6, [128, 4, 128], BF16)
                ZFr = Ring(S, "ZF", 2, [128, 8, 128], BF16)
                RTcr = Ring(S, "RTc", 2, [64, 8, 2, 128], BF16)
                Wcr = Ring(S, "Wc", 2, [64, 8, 2, 64], BF16)
                H32 = S.sb("H32", [64, 8, 64], F32)
                Hbf = S.sb("Hbf", [64, 8, 64], BF16)
                tmpH = S.sb("tmpH", [64, 8, 64], F32)
                ytr = Ring(S, "ytile", 3, [128, 512], F32)
                fsq = tn["tmp"]; fyn = S.sb("fyn", [128, 512], F32); fbv = S.sb("fbv", [128, 512], BF16)
                print("phase R sbuf bytes remaining", nc.sbuf_bytes_remaining)
                fst = S.sb("fst", [128, 6, 8], F32)
                fob = S.sb("fob", [128, 512], BF16)
                rwo = Ring(S, "rwo", 2, [128, 4, 128], BF16)
                rwT_v = rwT_d.rearrange("(k p) t -> p k t", p=128)
                nstR = [0]
                v3 = lambda ap: ap.rearrange("p (c t) -> p c t", t=64)
                v4 = lambda ap: ap.rearrange("p (j t) -> p j t", t=128)
                identbc = identb[:].unsqueeze(1).to_broadcast([128, 4, 128])

                def gen_prep(d, s, blk, B):
                    l0 = blk * 512
                    cp = ppos(s) + l0
                    AR, ARk = B["AR"]; BT, BTk = B["BT"]; KT, KTk = B["KT"]; BH, BHk = B["BH"]; KH, KHk = B["KH"]; VT, VTk = B["VT"]
                    GCt, GCk = B["GC"]
                    RK, RKk = B["RK"]
                    sgd, sgdk = B["sgd"]
                    for i in range(14):
                        if i % 4 == 0:
                            p_full, p_k = pin.next()
                            nch = min(4, 14 - i)
                            S.dma("sp", p_full[:, 0:nch, :], pR_d[i * 128:(i + nch) * 128, cp:cp + 514].rearrange("(i p) t -> p i t", p=128), writes=[p_k])
                        ta, tb = (("sg", "E"), ("kk", "rs"))[i % 2]
                        if i < 8:
                            dst, dk = xs[:, i, :], "xs%d" % i
                        elif i < 12:
                            dst, dk = VT[:, i - 8, :], VTk
                        elif i == 12:
                            dst, dk = tn["Gi"][:], "Gi"
                        else:
                            dst, dk = tn["Dd"][:], "Dd"
                        S.pool(lambda e: e.tensor_tensor(tn[ta][:], p_full[:, i % 4, 0:512], p_full[:, i % 4, 2:514], ALU.add), reads=[p_k], writes=[ta])
                        S.dve(lambda e: e.scalar_tensor_tensor(tn[tb][:], tn[ta][:], 0.5, p_full[:, i % 4, 1:513], ALU.mult, ALU.subtract), reads=[ta, p_k], writes=[tb])
                        if i < 8:
                            dst, dk = xs[:, i, :], "xs%d" % i
                        elif i < 12:
                            dst, dk = VT[:, i - 8, :], VTk
                        elif i == 12:
                            dst, dk = tn["Gi"][:], "Gi"
                        else:
                            dst, dk = tn["Dd"][:], "Dd"
                        S.dve(lambda e: e.scalar_tensor_tensor(dst, tn[tb][:], vecs[:, V_MU + i:V_MU + i + 1], p_full[:, i % 4, 1:513], ALU.mult, ALU.add),
                              reads=[tb, "vecs", p_k], writes=[dk])
                        if i % 4 == 3:
                            yield
                    S.act(lambda e: e.activation(twsa[0:64, :], tn["Gi"][0:64, :], AF.Tanh), reads=["Gi"], writes=["twsa"])
                    S.pool(lambda e: e.tensor_copy(twsa[64:128, :], tn["Gi"][64:128, :]), reads=["Gi"], writes=["twsa"])
                    if d == 1:
                        S.act(lambda e: e.activation(sgd[:], tn["Dd"][:], AF.Sigmoid), reads=["Dd"], writes=[sgdk])
                    yield
                    yield
                    yield
                    for pr in range(4):
                        xr, xk = xs[:, pr, :], xs[:, 4 + pr, :]
                        xrk, xkk = "xs%d" % pr, "xs%d" % (4 + pr)
                        pz, pzk = PS4.next()
                        S.pe(lambda e: e.matmul(pz[:, :], wdec[0:64, d, pr * 128:(pr + 1) * 128], twsa[0:64, :], start=True, stop=True), reads=["wdec", "twsa"], writes=[pzk], rg=0)
                        S.act(lambda e: e.activation(tn["sg"][:], pz[:, :], AF.Sigmoid, bias=vecs[:, V_W0 + d * 4 + pr:V_W0 + d * 4 + pr + 1]), reads=[pzk, "vecs"], writes=["sg"])
                        pa_, pak = PS4.next()
                        S.pe(lambda e: e.matmul(pa_[:, :], wicl[64:128, d, pr * 128:(pr + 1) * 128], twsa[64:128, :], start=True, stop=True), reads=["wicl", "twsa"], writes=[pak], rg=64)
                        S.act(lambda e: e.activation(tn["aa"][:], pa_[:, :], AF.Sigmoid, bias=vecs[:, V_A0 + d * 4 + pr:V_A0 + d * 4 + pr + 1]), reads=[pak, "vecs"], writes=["aa"])
                        S.dve(lambda e: e.tensor_scalar(tn["kk"][:], xk, vecs[:, V_KK + pr:V_KK + pr + 1], None, ALU.mult), reads=[xkk, "vecs"], writes=["kk"])
                        S.act(lambda e: e.activation(sqk[:], tn["kk"][:], AF.Square), reads=["kk"], writes=["sqk"])
                        yield
                        yield
                        pq, pqk = PS4.next()
                        S.pe(lambda e: e.matmul(pq[:, :], BD1[:], sqk[:], start=True, stop=True), reads=["BD1", "sqk"], writes=[pqk])
                        S.dve(lambda e: e.tensor_tensor_scan(tn["E"][:], msk01[:].rearrange("p c t -> p (c t)"), tn["sg"][:], 0.0, ALU.mult, ALU.add), reads=["msk01", "sg"], writes=["E"])
                        Ek = "E"
                        if d == 1:
                            S.dve(lambda e: e.tensor_tensor(tn["E2"][:], tn["sg"][:], tn["E"][:], ALU.subtract), reads=["sg", "E"], writes=["E2"])
                            S.dve(lambda e: e.tensor_tensor(v3(tn["E2"][:]), v3(tn["E2"][:]), v3(tn["E"][:])[:, :, 63:64].to_broadcast([128, 8, 64]), ALU.add), reads=["E2", "E"], writes=["E2"])
                            Ek = "E2"
                        Et = tn[Ek]
                        ecol = 63 if d == 0 else 0
                        Etot = v3(Et[:])[:, :, ecol:ecol + 1]
                        S.act(lambda e: e.activation(tn["rs"][:], pq[:, :], AF.Ln, bias=1e-12), reads=[pqk], writes=["rs"])
                        yield
                        S.pool(lambda e: e.tensor_tensor(tn["tmp"][:], Et[:], tn["sg"][:], ALU.subtract), reads=[Ek, "sg"], writes=["tmp"])
                        S.act(lambda e: e.activation(tn["Gx"][:], tn["tmp"][:], AF.Exp, scale=-C0), reads=["tmp"], writes=["Gx"])
                        S.act(lambda e: e.activation(tn["G"][:], Et[:], AF.Exp, scale=-C0), reads=[Ek], writes=["G"])
                        S.act(lambda e: e.activation(tn["Gi"][:], Et[:], AF.Exp, scale=C0), reads=[Ek], writes=["Gi"])
                        S.pool(lambda e: e.tensor_tensor(v3(tn["tmp"][:]), Etot.to_broadcast([128, 8, 64]), v3(Et[:]), ALU.subtract), reads=[Ek, "tmp"], writes=["tmp"])
                        S.act(lambda e: e.activation(tn["Dd"][:], tn["tmp"][:], AF.Exp, scale=-C0), reads=["tmp"], writes=["Dd"])
                        S.act(lambda e: e.activation(GEnd[:, pr, :].unsqueeze(2), Etot, AF.Exp, scale=-C0), reads=[Ek], writes=["GEnd"])
                        S.act(lambda e: e.activation(tn["rs"][:], tn["rs"][:], AF.Exp, scale=-0.5), reads=["rs"], writes=["rs"])
                        S.pool(lambda e: e.tensor_tensor(tn["kkn"][:], tn["kk"][:], tn["rs"][:], ALU.mult), reads=["kk", "rs"], writes=["kkn"])
                        S.dve(lambda e: e.tensor_scalar(tn["tt"][:], tn["aa"][:], vecs[:, V_KA + pr:V_KA + pr + 1], omka[:, pr:pr + 1], ALU.mult, ALU.add), reads=["aa", "vecs", "omka"], writes=["tt"])
                        S.pool(lambda e: e.tensor_tensor(tn["kd"][:], xk, tn["tt"][:], ALU.mult), reads=[xkk, "tt"], writes=["kd"])
                        S.pool(lambda e: e.tensor_tensor(tn["ka"][:], tn["kkn"][:], tn["aa"][:], ALU.mult), reads=["kkn", "aa"], writes=["ka"])
                        yield
                        S.dve(lambda e: e.scalar_tensor_tensor(AR[:, pr, :, 0, :], v4(tn["kkn"][:]), -1.0, v4(tn["Gx"][:]), ALU.mult, ALU.mult), reads=["kkn", "Gx"], writes=[ARk])
                        S.dve(lambda e: e.tensor_tensor(AR[:, pr, :, 1, :], v4(xr), v4(tn["G"][:]), ALU.mult), reads=[xrk, "G"], writes=[ARk])
                        S.dve(lambda e: e.tensor_tensor(BT[:, pr, :], tn["ka"][:], tn["Gi"][:], ALU.mult), reads=["ka", "Gi"], writes=[BTk])
                        S.pool(lambda e: e.tensor_tensor(KT[:, pr, :], tn["kd"][:], tn["Gi"][:], ALU.mult), reads=["kd", "Gi"], writes=[KTk])
                        S.dve(lambda e: e.tensor_tensor(BH[:, pr, :], tn["ka"][:], tn["Dd"][:], ALU.mult), reads=["ka", "Dd"], writes=[BHk])
                        S.pool(lambda e: e.tensor_tensor(KH[:, pr, :], tn["kd"][:], tn["Dd"][:], ALU.mult), reads=["kd", "Dd"], writes=[KHk])
                        if d == 1:
                            pa2, pa2k = PS4.next()
                            S.pe(lambda e: e.matmul(pa2[:, :], wicl[64:128, 0, pr * 128:(pr + 1) * 128], twsa[64:128, :], start=True, stop=True), reads=["wicl", "twsa"], writes=[pa2k], rg=64)
                            S.act(lambda e: e.activation(tn["ao_"][:], pa2[:, :], AF.Sigmoid, bias=vecs[:, V_A0 + pr:V_A0 + pr + 1]), reads=[pa2k, "vecs"], writes=["ao_"])
                            S.dve(lambda e: e.tensor_scalar(tn["tt"][:], tn["ao_"][:], vecs[:, V_KA + pr:V_KA + pr + 1], omka[:, pr:pr + 1], ALU.mult, ALU.add), reads=["ao_", "vecs", "omka"], writes=["tt"])
                            S.pool(lambda e: e.tensor_tensor(tn["kdo"][:], xk, tn["tt"][:], ALU.mult), reads=[xkk, "tt"], writes=["kdo"])
                            S.pool(lambda e: e.tensor_tensor(tn["kdo"][:], tn["kdo"][:], tn["kd"][:], ALU.add), reads=["kdo", "kd"], writes=["kdo"])
                            S.dve(lambda e: e.scalar_tensor_tensor(RK[:, pr, :], xr, vecs[:, V_RK + pr:V_RK + pr + 1], tn["kdo"][:], ALU.mult, ALU.mult), reads=[xrk, "vecs", "kdo"], writes=[RKk])
                        yield
                    S.dve(lambda e: e.tensor_copy(ghi[:], GEnd[:]), reads=["GEnd"], writes=["ghi"])
                    S.dve(lambda e: e.tensor_tensor(gdf[:], GEnd[:], ghi[:], ALU.subtract), reads=["GEnd", "ghi"], writes=["gdf"])
                    S.dve(lambda e: e.tensor_copy(glo[:], gdf[:]), reads=["gdf"], writes=["glo"])
                    yield
                    yield
                    pg_, pgk = PS4.next()
                    for h in (0, 2, 4, 6, 1, 3, 5, 7):
                        sl = slice(64 * (h % 2), 64 * (h % 2) + 64)
                        S.pe(lambda e: e.matmul(pg_[0:64, h * 8:(h + 1) * 8], identb[sl, sl], ghi[sl, h // 2, :], start=True, stop=False), reads=["identb", "ghi"], writes=[pgk], rg=64 * (h % 2))
                        S.pe(lambda e: e.matmul(pg_[0:64, h * 8:(h + 1) * 8], identb[sl, sl], glo[sl, h // 2, :], start=False, stop=True), reads=["identb", "glo"], writes=[pgk], rg=64 * (h % 2))
                    S.act(lambda e: e.activation(GCt[:].rearrange("p h c -> p (h c)"), pg_[0:64, 0:64], AF.Copy), reads=[pgk], writes=[GCk])
                    yield

                def gen_group(d, X, q):
                    j, tsl = X["j"], X["tsl"]
                    B = X["B"]
                    AR, ARk = B["AR"]; BT, BTk = B["BT"]; KT, KTk = B["KT"]; BH, BHk = B["BH"]; KH, KHk = B["KH"]; VT, VTk = B["VT"]
                    s12, s12k = X["s12"]; TOK, TOKk = X["TOK"]; ZF, ZFk = X["ZF"]; RTc, RTck = X["RTc"]; Wc, Wck = X["Wc"]; psY, psYk = X["psY"]
                    hs = [4 * q + u for u in range(4)]
                    prs = [h // 2 for h in hs]
                    sls = [slice(64 * (h % 2), 64 * (h % 2) + 64) for h in hs]
                    for u, h in enumerate(hs):
                        b1, b1k = PS4.next()
                        arf = AR[sls[u], prs[u], j].rearrange("p a t -> p (a t)")
                        S.pe(lambda e: e.matmul(b1[:, 0:256], BT[sls[u], prs[u], tsl], arf, start=True, stop=True), reads=[BTk, ARk], writes=[b1k], rg=64 * (h % 2))
                        S.pe(lambda e: e.matmul(b1[:, 256:512], KT[sls[u], prs[u], tsl], arf, start=True, stop=True), reads=[KTk, ARk], writes=[b1k], rg=64 * (h % 2))
                        S.dve(lambda e: e.tensor_tensor(s12[:, h, :], b1[:, :], M12[d][:].rearrange("p a t -> p (a t)"), ALU.mult), reads=[b1k, "M12"], writes=[s12k])
                        if u == 1:
                            yield
                    b3, b3k = PS4.next()
                    for u in (0, 2, 1, 3):
                        h = hs[u]
                        S.pe(lambda e: e.matmul(b3[:, u * 128:(u + 1) * 128], AR[sls[u], prs[u], j, 0, :], BT[sls[u], prs[u], tsl], start=True, stop=True), reads=[ARk, BTk], writes=[b3k], rg=64 * (h % 2))
                    Rc, Rck = Rr.next()
                    S.dve(lambda e: e.tensor_tensor(Rc[:], v4(b3[:, :]), M3[d][:].unsqueeze(1).to_broadcast([128, 4, 128]), ALU.mult), reads=[b3k, M3k[d]], writes=[Rck])
                    bT, bTk = PS4.next()
                    bTb = bT[:].bitcast(BF16)
                    for u in (0, 2, 1, 3):
                        h = hs[u]
                        srcs = ((AR[sls[u], prs[u], j, 0, :], ARk), (VT[sls[u], prs[u], tsl], VTk), (BH[sls[u], prs[u], tsl], BHk), (KH[sls[u], prs[u], tsl], KHk))
                        for i_, (src, sk) in enumerate(srcs):
                            S.pe(lambda e: e.transpose(bTb[:, u * 256 + i_ * 64:u * 256 + i_ * 64 + 64], src, identb[sls[u], sls[u]]), reads=[sk, "identb"], writes=[bTk], rg=64 * (h % 2))
                    S.act(lambda e: e.activation(TOK[:, 4 * q:4 * q + 4, :].rearrange("p u c -> p (u c)"), bTb[:, 0:1024], AF.Copy), reads=[bTk], writes=[TOKk])
                    yield
                    Pc, Pck = s12[:, 4 * q:4 * q + 4, 0:128], s12k
                    Zc, Zck = Zr.next()
                    S.pool(lambda e: e.tensor_copy(Zc[:, :, 0:64], TOK[:, 4 * q:4 * q + 4, 0:64]), reads=[TOKk], writes=[Zck])
                    bA, bAk = PS4.next()
                    for u, h in enumerate(hs):
                        S.pe(lambda e: e.matmul(bA[:, u * 64:(u + 1) * 64], s12[:, h, 256:384], TOK[:, h, 64:128], start=True, stop=True), reads=[s12k, TOKk], writes=[bAk])
                    S.act(lambda e: e.activation(Zc[:, :, 64:128], bA[:, 0:256].rearrange("p (u c) -> p u c", c=64), AF.Copy), reads=[bAk], writes=[Zck])
                    yield
                    for l in range(6):
                        if l < 5:
                            bP, bPk = PS4.next()
                            for u in range(4):
                                S.pe(lambda e: e.matmul(bP[:, u * 128:(u + 1) * 128], Rc[:, u, :], Pc[:, u, :], start=True, stop=True), reads=[Rck, Pck], writes=[bPk])
                            if l < 4:
                                bR, bRk = PS4.next()
                                for u in range(4):
                                    S.pe(lambda e: e.matmul(bR[:, u * 128:(u + 1) * 128], Pc[:, u, :], Rc[:, u, :], start=True, stop=True), reads=[Rck, Pck], writes=[bRk])
                        bZ, bZk = PS4.next()
                        S.pe(lambda e: e.matmul(bZ[:, :], identb[:], Zc[:].rearrange("p u c -> p (u c)"), start=True, stop=False), reads=["identb", Zck], writes=[bZk])
                        for u in range(4):
                            S.pe(lambda e: e.matmul(bZ[:, u * 128:(u + 1) * 128], Pc[:, u, :], Zc[:, u, :], start=False, stop=True), reads=[Pck, Zck], writes=[bZk])
                        Pn = None
                        if l < 5:
                            Pn, Pnk = Pr.next()
                            S.act(lambda e: e.activation(Pn[:], v4(bP[:, :]), AF.Copy), reads=[bPk], writes=[Pnk])
                            if l < 4:
                                Rn, Rnk = Rr.next()
                                S.act(lambda e: e.activation(Rn[:], v4(bR[:, :]), AF.Copy), reads=[bRk], writes=[Rnk])
                                Rc, Rck = Rn, Rnk
                            Zn, Znk = Zr.next()
                            if l % 2 == 0:
                                S.dve(lambda e: e.tensor_copy(Zn[:], v4(bZ[:, :])), reads=[bZk], writes=[Znk])
                            else:
                                S.act(lambda e: e.activation(Zn[:], v4(bZ[:, :]), AF.Copy), reads=[bZk], writes=[Znk])
                            Zc, Zck = Zn, Znk
                            Pc, Pck = Pn[:], Pnk
                        else:
                            S.dve(lambda e: e.tensor_copy(ZF[:, 4 * q:4 * q + 4, :], v4(bZ[:, :])), reads=[bZk], writes=[ZFk])
                        yield
                    bRT, bRTk = PS4.next()
                    for u, h in enumerate(hs):
                        S.pe(lambda e: e.matmul(bRT[0:64, u * 128:(u + 1) * 128], identb[sls[u], sls[u]], AR[sls[u], prs[u], j, 1, :], start=True, stop=False), reads=["identb", ARk], writes=[bRTk], rg=64 * (h % 2))
                        S.pe(lambda e: e.matmul(bRT[0:64, u * 128:(u + 1) * 128], ZF[:, h, 0:64], s12[:, h, 128:256], start=False, stop=True), reads=[ZFk, s12k], writes=[bRTk])
                    S.dve(lambda e: e.tensor_tensor(RTc[:, 4 * q:4 * q + 4, :, :], v4(bRT[0:64, :]).unsqueeze(2).to_broadcast([64, 4, 2, 128]),
                                                    CM[:].unsqueeze(1).to_broadcast([64, 4, 2, 128]), ALU.mult), reads=[bRTk, "CM"], writes=[RTck])
                    bW, bWk = PS4.next()
                    for c in range(2):
                        for u, h in enumerate(hs):
                            S.pe(lambda e: e.matmul(bW[0:64, (u * 2 + c) * 64:(u * 2 + c + 1) * 64], ZF[64 * c:64 * c + 64, h, 0:64], TOK[64 * c:64 * c + 64, h, 128:192], start=True, stop=True),
                                 reads=[ZFk, TOKk], writes=[bWk], rg=64 * c)
                    S.act(lambda e: e.activation(Wc[:, 4 * q:4 * q + 4, :, :].rearrange("p u c k -> p (u c k)"), bW[0:64, :], AF.Copy), reads=[bWk], writes=[Wck])
                    for u, h in enumerate(hs):
                        S.pe(lambda e: e.matmul(psY[:, h * 64:(h + 1) * 64], s12[:, h, 128:256], ZF[:, h, 64:128], start=X["firstY"], stop=False, skip_group_check=True), reads=[s12k, ZFk], writes=[psYk])
                        X["firstY"] = False
                        S.pe(lambda e: e.matmul(psY[:, h * 64:(h + 1) * 64], s12[:, h, 384:512], TOK[:, h, 64:128], start=False, stop=False, skip_group_check=True), reads=[s12k, TOKk], writes=[psYk])
                    yield

                def gen_chain(d, X, B):
                    j, tsl, g0 = X["j"], X["tsl"], X["g0"]
                    TOK, TOKk = X["TOK"]; ZF, ZFk = X["ZF"]; RTc, RTck = X["RTc"]; Wc, Wck = X["Wc"]; psY, psYk = X["psY"]
                    GCt, GCk = B["GC"]
                    RK, RKk = B["RK"]
                    sgd, sgdk = B["sgd"]
                    corder = (0, 1) if d == 0 else (1, 0)
                    for ci, c in enumerate(corder):
                        cg = j * 2 + c
                        for h in range(8):
                            S.pe(lambda e: e.matmul(psY[:, h * 64:(h + 1) * 64], RTc[:, h, c, :], Hbf[:, h, :], start=False, stop=(ci == 1), skip_group_check=True), reads=[RTck, "Hbf"], writes=[psYk], rg=0)
                        bH, bHk = PH.next()
                        csl = slice(64 * c, 64 * c + 64)
                        for h in range(8):
                            S.pe(lambda e: e.matmul(bH[0:64, h * 64:(h + 1) * 64], Wc[:, h, c, :], Hbf[:, h, :], start=(h == 0), stop=False, skip_group_check=True), reads=[Wck, "Hbf"], writes=[bHk], rg=0)
                        for h in range(8):
                            S.pe(lambda e: e.matmul(bH[0:64, h * 64:(h + 1) * 64], TOK[csl, h, 128:192], ZF[csl, h, 64:128], start=False, stop=False, skip_group_check=True), reads=[TOKk, ZFk], writes=[bHk], rg=64 * c)
                            S.pe(lambda e: e.matmul(bH[0:64, h * 64:(h + 1) * 64], TOK[csl, h, 192:256], TOK[csl, h, 64:128], start=False, stop=True, skip_group_check=True), reads=[TOKk], writes=[bHk], rg=64 * c)
                        S.dve(lambda e: e.tensor_tensor(tmpH[:], H32[:], GCt[:, :, cg:cg + 1].to_broadcast([64, 8, 64]), ALU.mult), reads=["H32", GCk], writes=["tmpH"])
                        yield
                        S.dve(lambda e: e.tensor_tensor(H32[:], tmpH[:], bH[0:64, :].rearrange("p (h v) -> p h v", v=64), ALU.add), reads=["tmpH", bHk], writes=["H32"])
                        S.act(lambda e: e.activation(Hbf[:], H32[:], AF.Copy), reads=["H32"], writes=["Hbf"])
                        yield
                    if d == 0:
                        y_t, y_k = ytr.next()
                        S.act(lambda e: e.activation(y_t[:], psY[:, :], AF.Copy), reads=[psYk], writes=[y_k])
                        nstR[0] += 1
                        S.dma("pool", y0_d[g0 + j * 128:g0 + (j + 1) * 128, :], y_t[:], reads=[y_k], writes=["y0scr%d" % nstR[0]])
                    else:
                        y0_t, y0_k = X["y0"]
                        y_t, y_k = ytr.next()
                        S.dve(lambda e: e.tensor_tensor(y_t[:], psY[:, :], y0_t[:], ALU.add), reads=[psYk, y0_k], writes=[y_k])
                        yv = y_t[:].rearrange("p (h v) -> p h v", v=64)
                        S.dve(lambda e: e.tensor_reduce(fst[:, 0, :], yv, AX.X, ALU.add), reads=[y_k], writes=["fst"])
                        S.act(lambda e: e.activation(fsq[:], y_t[:], AF.Square), reads=[y_k], writes=["tmp"])
                        S.dve(lambda e: e.tensor_reduce(fst[:, 1, :], fsq[:].rearrange("p (h v) -> p h v", v=64), AX.X, ALU.add), reads=["tmp"], writes=["fst"])
                        S.dve(lambda e: e.tensor_scalar(fst[:, 2, :], fst[:, 0, :], 1.0 / 64, None, ALU.mult), reads=["fst"], writes=["fst"])
                        S.dve(lambda e: e.tensor_tensor(fst[:, 3, :], fst[:, 2, :], fst[:, 2, :], ALU.mult), reads=["fst"], writes=["fst"])
                        S.dve(lambda e: e.scalar_tensor_tensor(fst[:, 4, :], fst[:, 1, :], 1.0 / 64, fst[:, 3, :], ALU.mult, ALU.subtract), reads=["fst"], writes=["fst"])
                        S.act(lambda e: e.activation(fst[:, 4, :], fst[:, 4, :], AF.Sqrt, bias=LNX_EPS), reads=["fst"], writes=["fst"])
                        S.dve(lambda e: e.reciprocal(fst[:, 4, :], fst[:, 4, :]), reads=["fst"], writes=["fst"])
                        yield
                        fynv = fyn[:].rearrange("p (h v) -> p h v", v=64)
                        S.dve(lambda e: e.tensor_tensor(fynv, yv, fst[:, 2, :].unsqueeze(2).to_broadcast([128, 8, 64]), ALU.subtract), reads=[y_k, "fst"], writes=["fyn"])
                        S.dve(lambda e: e.tensor_tensor(fynv, fynv, fst[:, 4, :].unsqueeze(2).to_broadcast([128, 8, 64]), ALU.mult), reads=["fyn", "fst"], writes=["fyn"])
                        S.pool(lambda e: e.tensor_tensor(fyn[:], fyn[:], lnwb[:, 0, :], ALU.mult), reads=["fyn", "lnwb"], writes=["fyn"])
                        S.pool(lambda e: e.tensor_tensor(fyn[:], fyn[:], lnwb[:, 1, :], ALU.add), reads=["fyn", "lnwb"], writes=["fyn"])
                        pbn, pbnk = PS4.next()
                        for pr in range(4):
                            S.pe(lambda e: e.matmul(pbn[:, 0:8], RK[:, pr, tsl], HIND[:, pr, :], start=(pr == 0), stop=(pr == 3)), reads=[RKk, "HIND"], writes=[pbnk])
                        S.act(lambda e: e.activation(fst[:, 5, :], pbn[:, 0:8], AF.Copy), reads=[pbnk], writes=["fst"])
                        S.dve(lambda e: e.tensor_tensor(fbv[:].rearrange("p (h v) -> p h v", v=64), TOK[:, :, 64:128], fst[:, 5, :].unsqueeze(2).to_broadcast([128, 8, 64]), ALU.mult), reads=[TOKk, "fst"], writes=["fbv"])
                        S.pool(lambda e: e.tensor_tensor(fyn[:], fyn[:], fbv[:], ALU.add), reads=["fyn", "fbv"], writes=["fyn"])
                        yield
                        pgt, pgtk = PS4.next()
                        S.pe(lambda e: e.matmul(pgt[:, :], sgd[:, tsl], wgat[:], start=True, stop=True), reads=[sgdk, "wgat"], writes=[pgtk])
                        S.dve(lambda e: e.tensor_tensor(fob[:], fyn[:], pgt[:, :], ALU.mult), reads=["fyn", pgtk], writes=["fob"])
                        bT2, bT2k = PS4.next()
                        bT2b = bT2[:].bitcast(BF16)
                        for kk_ in range(4):
                            S.pe(lambda e: e.transpose(bT2b[:, kk_ * 128:(kk_ + 1) * 128], fob[:, kk_ * 128:(kk_ + 1) * 128], identb[:]), reads=["fob", "identb"], writes=[bT2k])
                        ro_t, ro_k = rwo.next()
                        S.act(lambda e: e.activation(ro_t[:].rearrange("p k t -> p (k t)"), bT2b[:, 0:512], AF.Copy), reads=[bT2k], writes=[ro_k])
                        nstR[0] += 1
                        S.dma("pool", rwT_v[:, :, g0 + j * 128:g0 + (j + 1) * 128], ro_t[:], reads=[ro_k], writes=["rwscr%d" % nstR[0]])
                    yield

                def run_rr(gens, bg=None, chain=None, chain_delay=3, bg_hold=False):
                    gens = [g_ for g_ in gens if g_ is not None]
                    rnd = 0
                    while gens or chain is not None:
                        for g_ in list(gens):
                            try:
                                next(g_)
                            except StopIteration:
                                gens.remove(g_)
                        if chain is not None and (rnd >= chain_delay or not gens):
                            try:
                                next(chain)
                            except StopIteration:
                                chain = None
                        if bg is not None and bg[0] is not None and (chain is None or not bg_hold):
                            try:
                                next(bg[0])
                            except StopIteration:
                                bg[0] = None
                        rnd += 1

                def new_block():
                    return {"AR": ARr.next(), "BT": BTr.next(), "KT": KTr.next(), "BH": BHr.next(), "KH": KHr.next(), "VT": VTr.next(),
                            "GC": GCr.next(), "RK": RKr.next(), "sgd": sgdr.next()}

                for d in range(2):
                    for s in range(3):
                        T = seqT[s]
                        nblk = T // 512
                        S.pool(lambda e: e.memset(H32[:], 0.0), reads=["H32"], writes=["H32"])
                        S.pool(lambda e: e.memset(Hbf[:], 0.0), reads=["Hbf"], writes=["Hbf"])
                        prev_chain = None
                        order = [bi if d == 0 else nblk - 1 - bi for bi in range(nblk)]
                        Bcur = new_block()
                        run_rr([gen_prep(d, s, order[0], Bcur)])
                        for bi in range(nblk):
                            blk = order[bi]
                            g0 = soff[s] + blk * 512
                            B = Bcur
                            bg = [None]
                            if bi + 1 < nblk:
                                Bcur = new_block()
                                bg[0] = gen_prep(d, s, order[bi + 1], Bcur)
                            for ji in range(4):
                                j = ji if d == 0 else 3 - ji
                                X = {"j": j, "tsl": slice(j * 128, (j + 1) * 128), "g0": g0, "firstY": True, "B": B,
                                     "s12": sb12.next(), "TOK": TOKr.next(), "ZF": ZFr.next(), "RTc": RTcr.next(), "Wc": Wcr.next(), "psY": PY.next()}
                                if d == 1:
                                    X["y0"] = ytr.next()
                                    S.dma("sp", X["y0"][0][:], y0_d[g0 + j * 128:g0 + (j + 1) * 128, :], writes=[X["y0"][1]])
                                run_rr([gen_group(d, X, 0), gen_group(d, X, 1)], bg=bg, chain=prev_chain, bg_hold=(ji == 0))
                                prev_chain = gen_chain(d, X, B)
                            if bg[0] is not None:
                                run_rr([bg[0]])
                        run_rr([prev_chain])
                    S.barrier()
                S.es = old

        if "C" in phases:
            use_rw = "R" in phases
            with ExitStack() as sc:
                S.es, old = sc, S.es
                NB = 256
                wmo = S.sb("wmo", [128, 4, D], BF16); wro = S.sb("wro", [128, 4, D], BF16)
                wout = S.sb("wout", [128, 8, D], BF16); wfi = S.sb("wfi", [128, 8, 2 * DFF], BF16)
                S.dma("sp", wmo[:], b_mo.rearrange("(k p) n -> p k n", p=128), writes=["wmo"])
                S.dma("sp", wro[:], b_ro.rearrange("(k p) n -> p k n", p=128), writes=["wro"])
                S.dma("sp", wout[:], b_out.rearrange("(k p) n -> p k n", p=128), writes=["wout"])
                b_fi_v = b_fi.rearrange("(k p) n -> p k n", p=128)
                for k in range(8):
                    S.dma("sp", wfi[:, k, :], b_fi_v[:, k, :], writes=["wfi"])
                wfo = Ring(S, "wfo", 3, [128, 22, 128], BF16)
                xtok = Ring(S, "xtok", 2, [128, D], F32)
                xT = S.sb("xT", [128, 8, NB], F32)
                att = Ring(S, "att", 2, [128, 4, NB], BF16)
                rwt = Ring(S, "rwt", 2, [128, 4, NB], BF16)
                gat = Ring(S, "gat", 1, [128, 16, NB], BF16)
                tmpC = Ring(S, "tmpC", 4, [128, NB], F32)
                mix = S.sb("mix", [128, 8, NB], BF16)
                sqc = S.sb("sqc", [128, 8, NB], BF16)
                rb2 = Ring(S, "rb2", 2, [128, NB], F32)
                h2 = S.sb("h2", [128, 8, NB], BF16)
                aff = S.sb("aff", [128, 22, NB], BF16)
                ytok = Ring(S, "ytok", 2, [128, D], F32)
                atT_v = atT_d.rearrange("(k p) t -> p k t", p=128)
                rwT_v = rwT_d.rearrange("(k p) t -> p k t", p=128)
                gT_v = gT_d.rearrange("(k p) t -> p k t", p=128)
                alt = [0]
                print("phase C sbuf bytes remaining", nc.sbuf_bytes_remaining)

                def rms_bc(r_t, r_k):
                    for k in range(8):
                        S.act(lambda e: e.activation(sqc[:, k, :], xT[:, k, :], AF.Square), reads=["xT%d" % k], writes=["sqc%d" % k])
                    pa, pak = PS.next()
                    for k in range(8):
                        S.pe(lambda e: e.matmul(pa[:, 0:NB], onesb[:], sqc[:, k, :], start=(k == 0), stop=(k == 7)), reads=["onesb", "sqc%d" % k], writes=[pak])
                    S.act(lambda e: e.activation(r_t[:], pa[:, 0:NB], AF.Sqrt, bias=EPS, scale=1.0 / D), reads=[pak], writes=[r_k])
                    S.dve(lambda e: e.reciprocal(r_t[:], r_t[:]), reads=[r_k], writes=[r_k])

                nyo = [0]
                for s in range(3):
                    for blk in range(seqT[s] // NB):
                        g0 = soff[s] + blk * NB
                        a_t, a_k = att.next()
                        S.dma("sp", a_t[:], atT_v[:, :, g0:g0 + NB], writes=[a_k])
                        w_t, w_k = rwt.next()
                        if use_rw:
                            S.dma("sp", w_t[:], rwT_v[:, :, g0:g0 + NB], writes=[w_k])
                        ga_t, ga_k = gat.next()
                        S.dma("sp", ga_t[:], gT_v[:, :, g0:g0 + NB], writes=[ga_k])
                        for j in range(NB // 128):
                            x_t, x_k = xtok.next()
                            S.dma("sp", x_t[:], xsrc(g0 + j * 128, 128), writes=[x_k])
                            for kq in range(2):
                                pb, pk = PS.next()
                                for kk in range(4):
                                    k = kq * 4 + kk
                                    S.pe(lambda e: e.transpose(pb[:, kk * 128:(kk + 1) * 128], x_t[:, k * 128:(k + 1) * 128], identf[:]), reads=[x_k, "identf"], writes=[pk])
                                cast_any(xT[:, kq * 4:kq * 4 + 4, j * 128:(j + 1) * 128], pb[:, :].rearrange("p (k t) -> p k t", t=128), [pk], ["xT%d" % k_ for k_ in range(kq * 4, kq * 4 + 4)], engines=("act", "dve"))
                        for m in range(8):
                            pa, pak = PS.next()
                            for kc in range(4):
                                S.pe(lambda e: e.matmul(pa[:, 0:NB], wmo[:, kc, m * 128:(m + 1) * 128], a_t[:, kc, :], start=(kc == 0), stop=(kc == 3)), reads=["wmo", a_k], writes=[pak])
                            t1, t1k = tmpC.next()
                            S.dve(lambda e: e.tensor_tensor(t1[:], pa[:, 0:NB], ga_t[:, m, :], ALU.mult), reads=[pak, ga_k], writes=[t1k])
                            if use_rw:
                                pb, pk = PS.next()
                                for kc in range(4):
                                    S.pe(lambda e: e.matmul(pb[:, 0:NB], wro[:, kc, m * 128:(m + 1) * 128], w_t[:, kc, :], start=(kc == 0), stop=(kc == 3)), reads=["wro", w_k], writes=[pk])
                                t2, t2k = tmpC.next()
                                S.dve(lambda e: e.tensor_tensor(t2[:], pb[:, 0:NB], ga_t[:, 8 + m, :], ALU.mult), reads=[pk, ga_k], writes=[t2k])
                                S.pool(lambda e: e.tensor_tensor(mix[:, m, :], t1[:], t2[:], ALU.add), reads=[t1k, t2k], writes=["mix%d" % m])
                            else:
                                S.pool(lambda e: e.tensor_copy(mix[:, m, :], t1[:]), reads=[t1k], writes=["mix%d" % m])
                        for m in range(8):
                            pa, pak = PS.next()
                            for k in range(8):
                                S.pe(lambda e: e.matmul(pa[:, 0:NB], wout[:, k, m * 128:(m + 1) * 128], mix[:, k, :], start=(k == 0), stop=(k == 7)), reads=["wout", "mix%d" % k], writes=[pak])
                            S.dve(lambda e: e.scalar_tensor_tensor(xT[:, m, :], pa[:, 0:NB], modv[:, 2, m, s:s + 1], xT[:, m, :], ALU.mult, ALU.add), reads=[pak, "modv", "xT%d" % m], writes=["xT%d" % m])
                        r_t, r_k = rb2.next()
                        rms_bc(r_t, r_k)
                        for k in range(8):
                            t1, t1k = tmpC.next()
                            S.dve(lambda e: e.tensor_tensor(t1[:], xT[:, k, :], r_t[:], ALU.mult), reads=["xT%d" % k, r_k], writes=[t1k])
                            S.dve(lambda e: e.tensor_scalar(h2[:, k, :], t1[:], modv[:, 3, k, s:s + 1], modv[:, 4, k, s:s + 1], ALU.mult, ALU.add), reads=[t1k, "modv"], writes=["h2_%d" % k])
                        for i in range(22):
                            pu, puk = PS.next()
                            pz, pzk = PS.next()
                            for k in range(8):
                                S.pe(lambda e: e.matmul(pu[:, 0:NB], wfi[:, k, i * 128:(i + 1) * 128], h2[:, k, :], start=(k == 0), stop=(k == 7)), reads=["wfi", "h2_%d" % k], writes=[puk])
                            for k in range(8):
                                S.pe(lambda e: e.matmul(pz[:, 0:NB], wfi[:, k, DFF + i * 128:DFF + (i + 1) * 128], h2[:, k, :], start=(k == 0), stop=(k == 7)), reads=["wfi", "h2_%d" % k], writes=[pzk])
                            t1, t1k = tmpC.next()
                            S.act(lambda e: e.activation(t1[:], pu[:, 0:NB], AF.Silu), reads=[puk], writes=[t1k])
                            S.dve(lambda e: e.tensor_tensor(aff[:, i, :], pz[:, 0:NB], t1[:], ALU.mult), reads=[pzk, t1k], writes=["aff%d" % i])
                        for m in range(8):
                            f_t, f_k = wfo.next()
                            S.dma("sp", f_t[:], b_fo[m], writes=[f_k])
                            pa, pak = PS.next()
                            for i in range(22):
                                S.pe(lambda e: e.matmul(pa[:, 0:NB], f_t[:, i, :], aff[:, i, :], start=(i == 0), stop=(i == 21)), reads=[f_k, "aff%d" % i], writes=[pak])
                            S.dve(lambda e: e.scalar_tensor_tensor(xT[:, m, :], pa[:, 0:NB], modv[:, 5, m, s:s + 1], xT[:, m, :], ALU.mult, ALU.add), reads=[pak, "modv", "xT%d" % m], writes=["xT%d" % m])
                        r_t, r_k = rb2.next()
                        rms_bc(r_t, r_k)
                        for k in range(8):
                            S.dve(lambda e: e.scalar_tensor_tensor(xT[:, k, :], xT[:, k, :], vecs[:, V_FN + k:V_FN + k + 1], r_t[:], ALU.mult, ALU.mult), reads=["xT%d" % k, "vecs", r_k], writes=["xT%d" % k])
                        for j in range(NB // 128):
                            y_t, y_k = ytok.next()
                            for kq in range(2):
                                pb, pk = PS.next()
                                for kk in range(4):
                                    k = kq * 4 + kk
                                    S.pe(lambda e: e.transpose(pb[:, kk * 128:(kk + 1) * 128], xT[:, k, j * 128:(j + 1) * 128], identf[:]), reads=["xT%d" % k, "identf"], writes=[pk])
                                cast_any(y_t[:, kq * 512:(kq + 1) * 512], pb[:, :], [pk], [y_k], engines=("act", "dve"))
                            nyo[0] += 1
                            S.dma("pool", ydst(g0 + j * 128, 128), y_t[:], reads=[y_k], writes=["yout%d" % nyo[0]])
                S.barrier()
                S.es = old

        S.finish()
    return nc


def _rope_tables(TM):
    inv = (np.float32(10000.0) ** (-(np.arange(16, dtype=np.float32)) / np.float32(16))).astype(np.float32)
    ang = (np.arange(TM, dtype=np.float32)[None, :] * inv[:, None]).astype(np.float32)
    idx = np.arange(128) % 16
    return np.stack([np.cos(ang)[idx], np.sin(ang)[idx]]).astype(np.float32)


def host_maps(inp, T0, T1, ncores):
    f = lambda a: np.ascontiguousarray(np.asarray(a, dtype=np.float32))
    g = {k: np.asarray(v) for k, v in inp.items()}
    vecs = np.zeros((128, NV), np.float32)

    def put(col, v):
        v = np.asarray(v, np.float32).reshape(-1, 128).T
        vecs[:, col:col + v.shape[1]] = v
    put(V_NMIX, g["norm_mix"][0]); put(V_NFFN, g["norm_ffn"][0]); put(V_FN, g["final_norm"]); put(V_BADA, g["b_ada"][0])
    put(V_QAN, g["q_a_norm"][0]); put(V_KVAN, g["kv_a_norm"][0]); put(V_MU, g["mu_shift"][0])
    put(V_W0, g["w0"][0].reshape(-1)); put(V_A0, g["a0"][0].reshape(-1))
    put(V_KK, g["k_k"][0]); put(V_KA, g["k_a"][0]); put(V_RK, g["r_k"][0].reshape(-1))
    w_in = g["w_in"][0]
    w_uq = g["w_uq"][0].reshape(QL, NH, 96)
    w_ukv = g["w_ukv"][0].reshape(KVL, NH, 128)
    shared = {
        "vecs": vecs,
        "lnwb": f(np.stack([g["lnx_w"][0], g["lnx_b"][0]])),
        "rope": _rope_tables(max(T0, T1)),
        "w_ada": f(g["w_ada"][0]), "w_in": f(w_in),
        "w_kps": f(np.concatenate([w_in[:, 656:672], w_in[:, 640:656]], axis=1)),
        "w_uqn": f(w_uq[:, :, :64].reshape(QL, 512)), "w_uqp": f(w_uq[:, :, 64:].reshape(QL, 256)),
        "w_uqs": f(np.concatenate([w_uq[:, :, 80:96], w_uq[:, :, 64:80]], axis=2).reshape(QL, 256)),
        "w_ukn": f(w_ukv[:, :, :64].reshape(KVL, 512)), "w_ukv": f(w_ukv[:, :, 64:].reshape(KVL, 512)),
        "w_dec": f(g["w_decay_up"][0]), "w_icl": f(g["w_iclr_up"][0]), "w_gat": f(g["w_gate_up"][0]),
        "w_mo": f(g["w_mla_o"][0]), "w_ro": f(g["w_rwkv_o"][0]), "w_out": f(g["w_out"][0]),
        "w_fi": f(g["w_ffn_in"][0]), "w_fo": f(g["w_ffn_out"][0]),
    }
    maps = []
    for c in range(ncores):
        c3 = np.concatenate([g["c_prompt"][c:c + 1], g["c_sample"][2 * c:2 * c + 2]], axis=0)
        cT = np.zeros((128, 8, 4), np.float32)
        cT[:, :, 0:3] = c3.reshape(3, 8, 128).transpose(2, 1, 0)
        m = dict(shared)
        m["x_p"] = f(g["x_prompt"][c])
        m["x_s"] = f(g["x_sample"][2 * c:2 * c + 2].reshape(2 * T1, D))
        m["cT"] = cT
        maps.append(m)
    return maps


_NC_CACHE = {}


def kernel(**inputs):
    T0 = inputs["x_prompt"].shape[1]
    T1 = inputs["x_sample"].shape[1]
    ncores = inputs["x_prompt"].shape[0]
    key = (T0, T1)
    if key not in _NC_CACHE:
        _NC_CACHE[key] = build_nc(T0, T1)
    nc = _NC_CACHE[key]
    maps = host_maps(inputs, T0, T1, ncores)
    res = run_bass_kernel_spmd(nc, maps, core_ids=list(range(ncores)))
    yp = np.stack([np.asarray(r["y_p"], np.float32) for r in res.results])
    ys = np.concatenate([np.asarray(r["y_s"], np.float32).reshape(2, T1, D) for r in res.results], axis=0)
    return (yp, ys)
```
